# Optimizing a Trainium2 kernel written in Bass

```python
import math
import jax, jax.numpy as jnp
from jax import lax
import numpy as np

D_MODEL = 1024
BATCH = 8
SEQ = 2048
DEPTH = 2

N_HEADS = 4
HEAD_DIM = 64
V_DIM = 2 * HEAD_DIM
QK_WIDTH = N_HEADS * 2 * HEAD_DIM
ATTN_WIDTH = N_HEADS * V_DIM
SSM_WIDTH = D_MODEL // 2
SSM_GROUP = 16
SSM_GROUPS = SSM_WIDTH // SSM_GROUP
SSM_STATE = 64
STEP_MIN = 1e-3
STEP_MAX = 1e-1
D_FF = -(-8 * D_MODEL // (3 * 256)) * 256
N_BUCKETS = 32
MAX_DISTANCE = 128
Q_BLOCK = 128
EPS = 1e-6
NEG = -1e30
IN_WIDTH = 2 * QK_WIDTH + ATTN_WIDTH + SSM_WIDTH + 2 * D_MODEL

kernel_name = "hybrid_diffattn_s5_gated_block"


def rmsnorm(x, g):
    xf = x.astype(jnp.float32)
    y = xf * lax.rsqrt(jnp.mean(xf * xf, axis=-1, keepdims=True) + EPS)
    return (y * g.astype(jnp.float32)).astype(x.dtype)


def t5_bucket(rel):
    n = jnp.maximum(rel, 0)
    max_exact = N_BUCKETS // 2
    is_small = n < max_exact
    nf = jnp.maximum(n, 1).astype(jnp.float32)
    large = max_exact + (jnp.log(nf / max_exact) / math.log(MAX_DISTANCE / max_exact)
                         * (N_BUCKETS - max_exact)).astype(jnp.int32)
    large = jnp.minimum(large, N_BUCKETS - 1)
    return jnp.where(is_small, n, large)


def diff_attention(q, k, v, lam, bias_table):
    B, H, _, S, Dh = q.shape
    nblk = S // Q_BLOCK
    scale = Dh ** -0.5
    kpos = jnp.arange(S)
    qb = q.reshape(B, H, 2, nblk, Q_BLOCK, Dh).transpose(3, 0, 1, 2, 4, 5)

    def block(args):
        qi, i = args
        qpos = i * Q_BLOCK + jnp.arange(Q_BLOCK)
        rel = qpos[:, None] - kpos[None, :]
        bias = bias_table[t5_bucket(rel)].astype(jnp.float32).transpose(2, 0, 1)
        s = jnp.einsum('bhmqd,bhmkd->bhmqk', qi, k,
                       preferred_element_type=jnp.float32) * scale + bias[None, :, None]
        s = jnp.where(rel[None, None, None] >= 0, s, NEG)
        p = jax.nn.softmax(s, axis=-1)
        a = p[:, :, 0] - lam * p[:, :, 1]
        return jnp.einsum('bhqk,bhkv->bhqv', a.astype(v.dtype), v)

    out = lax.map(block, (qb, jnp.arange(nblk)))
    return out.transpose(1, 2, 0, 3, 4).reshape(B, H, S, -1)


def s5_ssm(u, lam_re, lam_im, b_re, b_im, c_re, c_im, d_skip, log_step):
    Bsz, S, W = u.shape
    uf = u.astype(jnp.float32).reshape(Bsz, S, SSM_GROUPS, SSM_GROUP)
    step = jnp.exp(log_step.astype(jnp.float32))[:, None]
    lr = lam_re.astype(jnp.float32)
    li = lam_im.astype(jnp.float32)
    decay = jnp.exp(lr * step)
    ab_re = decay * jnp.cos(li * step)
    ab_im = decay * jnp.sin(li * step)
    nr = ab_re - 1.0
    ni = ab_im
    den = lr * lr + li * li
    f_re = (nr * lr + ni * li) / den
    f_im = (ni * lr - nr * li) / den
    br = b_re.astype(jnp.float32)
    bi = b_im.astype(jnp.float32)
    bb_re = f_re[..., None] * br - f_im[..., None] * bi
    bb_im = f_re[..., None] * bi + f_im[..., None] * br
    bu_re = jnp.einsum('gpc,bsgc->sbgp', bb_re, uf)
    bu_im = jnp.einsum('gpc,bsgc->sbgp', bb_im, uf)
    a_re = jnp.broadcast_to(ab_re[None, None], (S, 1, SSM_GROUPS, SSM_STATE))
    a_im = jnp.broadcast_to(ab_im[None, None], (S, 1, SSM_GROUPS, SSM_STATE))

    def combine(e1, e2):
        a1r, a1i, b1r, b1i = e1
        a2r, a2i, b2r, b2i = e2
        return (a2r * a1r - a2i * a1i,
                a2r * a1i + a2i * a1r,
                a2r * b1r - a2i * b1i + b2r,
                a2r * b1i + a2i * b1r + b2i)

    _, _, xr, xi = lax.associative_scan(combine, (a_re, a_im, bu_re, bu_im), axis=0)
    y = (jnp.einsum('gcp,sbgp->bsgc', c_re.astype(jnp.float32), xr)
         - jnp.einsum('gcp,sbgp->bsgc', c_im.astype(jnp.float32), xi))
    y = y.reshape(Bsz, S, W) + d_skip.astype(jnp.float32) * uf.reshape(Bsz, S, W)
    return y.astype(u.dtype)


def setup_inputs(seed: int = 0) -> dict:
    key = jax.random.key(seed)
    ks = iter(jax.random.split(key, 32))
    f32 = jnp.float32

    def nrm(shape, scale):
        return jax.random.normal(next(ks), shape, f32) * scale

    n = jnp.arange(SSM_STATE, dtype=f32)
    lam_re = -0.5 + nrm((DEPTH, SSM_GROUPS, SSM_STATE), 1e-3)
    lam_im = math.pi * n[None, None, :] + nrm((DEPTH, SSM_GROUPS, SSM_STATE), 1e-3)
    log_step = jax.random.uniform(next(ks), (DEPTH, SSM_GROUPS), f32,
                                  math.log(STEP_MIN), math.log(STEP_MAX))
    return {
        "x": nrm((BATCH, SEQ, D_MODEL), 1.0),
        "rel_bias": nrm((N_BUCKETS, N_HEADS), 0.5),
        "norm_mix": 1.0 + nrm((DEPTH, D_MODEL), 0.02),
        "w_in": nrm((DEPTH, D_MODEL, IN_WIDTH), D_MODEL ** -0.5),
        "q_gain": 1.0 + nrm((DEPTH, HEAD_DIM), 0.02),
        "k_gain": 1.0 + nrm((DEPTH, HEAD_DIM), 0.02),
        "lambda_q1": nrm((DEPTH, HEAD_DIM), 0.1),
        "lambda_k1": nrm((DEPTH, HEAD_DIM), 0.1),
        "lambda_q2": nrm((DEPTH, HEAD_DIM), 0.1),
        "lambda_k2": nrm((DEPTH, HEAD_DIM), 0.1),
        "subln": 1.0 + nrm((DEPTH, V_DIM), 0.02),
        "w_a": nrm((DEPTH, ATTN_WIDTH, D_MODEL), ATTN_WIDTH ** -0.5),
        "lam_re": lam_re,
        "lam_im": lam_im,
        "b_re": nrm((DEPTH, SSM_GROUPS, SSM_STATE, SSM_GROUP), (2 * SSM_GROUP) ** -0.5),
        "b_im": nrm((DEPTH, SSM_GROUPS, SSM_STATE, SSM_GROUP), (2 * SSM_GROUP) ** -0.5),
        "c_re": nrm((DEPTH, SSM_GROUPS, SSM_GROUP, SSM_STATE), (2 * SSM_STATE) ** -0.5),
        "c_im": nrm((DEPTH, SSM_GROUPS, SSM_GROUP, SSM_STATE), (2 * SSM_STATE) ** -0.5),
        "d_skip": nrm((DEPTH, SSM_WIDTH), 1.0),
        "log_step": log_step,
        "w_glu": nrm((DEPTH, SSM_WIDTH, SSM_WIDTH), SSM_WIDTH ** -0.5),
        "w_b": nrm((DEPTH, SSM_WIDTH, D_MODEL), SSM_WIDTH ** -0.5),
        "w_o": nrm((DEPTH, D_MODEL, D_MODEL), D_MODEL ** -0.5),
        "norm_ffn": 1.0 + nrm((DEPTH, D_MODEL), 0.02),
        "w1": nrm((DEPTH, D_MODEL, D_FF), D_MODEL ** -0.5),
        "w3": nrm((DEPTH, D_MODEL, D_FF), D_MODEL ** -0.5),
        "w2": nrm((DEPTH, D_FF, D_MODEL), D_FF ** -0.5),
    }


def reference(x, rel_bias, norm_mix, w_in, q_gain, k_gain, lambda_q1, lambda_k1,
              lambda_q2, lambda_k2, subln, w_a, lam_re, lam_im, b_re, b_im, c_re, c_im,
              d_skip, log_step, w_glu, w_b, w_o, norm_ffn, w1, w3, w2):
    B, S, _ = x.shape
    splits = [QK_WIDTH, 2 * QK_WIDTH, 2 * QK_WIDTH + ATTN_WIDTH,
              2 * QK_WIDTH + ATTN_WIDTH + SSM_WIDTH,
              2 * QK_WIDTH + ATTN_WIDTH + SSM_WIDTH + D_MODEL]
    for l in range(DEPTH):
        lam_init = 0.8 - 0.6 * math.exp(-0.3 * l)
        h = rmsnorm(x, norm_mix[l])
        z = h @ w_in[l]
        q, k, v, u, g_a, g_b = jnp.split(z, splits, axis=-1)
        q = rmsnorm(q.reshape(B, S, N_HEADS, 2, HEAD_DIM).transpose(0, 2, 3, 1, 4), q_gain[l])
        k = rmsnorm(k.reshape(B, S, N_HEADS, 2, HEAD_DIM).transpose(0, 2, 3, 1, 4), k_gain[l])
        v = v.reshape(B, S, N_HEADS, V_DIM).transpose(0, 2, 1, 3)
        lam = (jnp.exp(jnp.sum(lambda_q1[l].astype(jnp.float32) * lambda_k1[l].astype(jnp.float32)))
               - jnp.exp(jnp.sum(lambda_q2[l].astype(jnp.float32) * lambda_k2[l].astype(jnp.float32)))
               + lam_init)
        o = diff_attention(q, k, v, lam, rel_bias)
        o = rmsnorm(o, subln[l]) * (1.0 - lam_init)
        o = o.transpose(0, 2, 1, 3).reshape(B, S, ATTN_WIDTH)
        y_a = o @ w_a[l]
        s = s5_ssm(u, lam_re[l], lam_im[l], b_re[l], b_im[l], c_re[l], c_im[l],
                   d_skip[l], log_step[l])
        s = jax.nn.gelu(s)
        s = s * jax.nn.sigmoid(s @ w_glu[l])
        y_b = s @ w_b[l]
        mixed = jax.nn.sigmoid(g_a) * y_a + jax.nn.sigmoid(g_b) * y_b
        x = x + mixed @ w_o[l]
        h = rmsnorm(x, norm_ffn[l])
        x = x + (jax.nn.silu(h @ w1[l]) * (h @ w3[l])) @ w2[l]
    return x
```

```python
import math
import os
from contextlib import ExitStack

import numpy as np
import ml_dtypes

import concourse.bass as bass
import concourse.mybir as mybir
from concourse.bass_utils import run_bass_kernel_spmd

F32 = mybir.dt.float32
BF16 = mybir.dt.bfloat16
AF = mybir.ActivationFunctionType
ALU = mybir.AluOpType
AX = mybir.AxisListType

D = 1024
S = 2048
DEPTH = 2
NH = 4
DFF = 2816
INW = 4096
G = 32
EPS = 1e-6
TWO_PI = 2.0 * math.pi
NCH = 4
CH = 512
HW_ = 768
NEG_BIG = -60000.0


class _Stop(Exception):
    pass


KSTOP = float(os.environ.get("KSTOP", "999"))


class Prog:
    def __init__(self, nc, es):
        self.nc = nc
        self.es = es
        self.eng = {"pe": nc.tensor, "act": nc.scalar, "dve": nc.vector, "pool": nc.gpsimd, "sp": nc.sync}
        self.sem = {}
        self.cnt = {}
        for e in self.eng:
            self.sem[e] = es.enter_context(nc.semaphore("c_" + e))
            self.cnt[e] = 0
        self.dsem = {}
        self.dcnt = {}
        self.waited = {e: {} for e in self.eng}
        self.last_w = {}
        self.readers = {}
        self.pending = {}

    def _wait(self, e, dep):
        kind, name, val = dep
        if kind == "eng" and name == e and e == "pe":
            return
        key = (kind, name)
        if self.waited[e].get(key, 0) >= val:
            return
        self.waited[e][key] = val
        sem = self.sem[name] if kind == "eng" else self.dsem[name]
        self.eng[e].wait_ge(sem, val)

    def _check_pending(self, e, reads, writes):
        for e2, lst in self.pending.items():
            if e2 == e:
                continue
            for (r_, w_) in lst:
                if (set(w_) & (set(reads) | set(writes))) or (set(r_) & set(writes)):
                    raise RuntimeError("pending (non-signalled) access on %s conflicts with op on %s: %s %s" % (e2, e, r_, w_))

    def _collect(self, reads, writes):
        deps = []
        for k in reads:
            if k in self.last_w:
                deps.append(self.last_w[k])
        for k in writes:
            if k in self.last_w:
                deps.append(self.last_w[k])
            deps.extend(self.readers.get(k, []))
        return deps

    def _commit(self, dep, reads, writes):
        for k in reads:
            self.readers.setdefault(k, []).append(dep)
        for k in writes:
            self.last_w[k] = dep
            self.readers[k] = []

    def op(self, e, fn, reads=(), writes=(), signal=True):
        self._check_pending(e, reads, writes)
        for d in self._collect(reads, writes):
            self._wait(e, d)
        ins = fn()
        if not signal:
            self.pending.setdefault(e, []).append((tuple(reads), tuple(writes)))
            return ins
        self.cnt[e] += 1
        ins.then_inc(self.sem[e], 1)
        dep = ("eng", e, self.cnt[e])
        for (r_, w_) in self.pending.get(e, []):
            self._commit(dep, r_, w_)
        self.pending[e] = []
        self._commit(dep, reads, writes)
        return ins

    def dma(self, q, slot, out, in_, reads=(), writes=()):
        if slot not in self.dsem:
            self.dsem[slot] = self.es.enter_context(self.nc.semaphore("d_" + slot))
            self.dcnt[slot] = 0
        self._check_pending(q, reads, writes)
        for d in self._collect(reads, writes):
            self._wait(q, d)
        ins = self.eng[q].dma_start(out=out, in_=in_)
        self.dcnt[slot] += 16
        ins.then_inc(self.dsem[slot], 16)
        self._commit(("dma", slot, self.dcnt[slot]), reads, writes)

    def barrier(self):
        for e in self.eng:
            for e2 in self.eng:
                if e2 != e and self.cnt[e2] > 0:
                    self._wait(e, ("eng", e2, self.cnt[e2]))
            for slot, v in self.dcnt.items():
                self._wait(e, ("dma", slot, v))
        self.last_w = {}
        self.readers = {}

    def finish(self):
        for e in ("sp", "act", "dve", "pool", "pe"):
            for e2 in self.eng:
                if e2 != e and self.cnt[e2] > 0:
                    self._wait(e, ("eng", e2, self.cnt[e2]))
            for slot, v in self.dcnt.items():
                self._wait(e, ("dma", slot, v))


def t5_bucket_np(n):
    n = np.maximum(n, 0)
    nf = np.maximum(n, 1).astype(np.float32)
    large = 16 + (np.log(nf / np.float32(16)) / np.float32(math.log(8.0)) * np.float32(16)).astype(np.int32)
    large = np.minimum(large, 31)
    return np.where(n < 16, n, large)


def host_consts():
    c = {}
    c["ident_f"] = np.eye(128, dtype=np.float32)
    c["ident_b"] = np.eye(128, dtype=np.float32).astype(ml_dtypes.bfloat16)
    c["antiid_b"] = np.eye(128, dtype=np.float32)[::-1].copy().astype(ml_dtypes.bfloat16)
    c["ones_b"] = np.ones((128, 128), np.float32).astype(ml_dtypes.bfloat16)
    blk = np.zeros((128, 128), np.float32)
    blk[:64, :64] = 1
    blk[64:, 64:] = 1
    c["blk_b"] = blk.astype(ml_dtypes.bfloat16)
    m = (np.arange(8)[:, None] > np.arange(8)[None, :]).astype(np.float32)
    c["negmask"] = -np.repeat(np.repeat(m, 16, 0), 16, 1)
    sg = np.concatenate([-np.ones(64), np.ones(64)]).astype(np.float32)
    c["sgn"] = np.stack([sg, -sg], 1)
    c["jidx"] = np.tile(np.arange(256, dtype=np.float32)[None], (128, 1))
    jm = np.ones((128, 256), np.float32)
    jm[:, 0] = 0
    c["jmask"] = jm
    n = np.arange(HW_ + 128) - 127
    oh = np.zeros((32, HW_ + 128), np.float32)
    bk = t5_bucket_np(n)
    for i, d in enumerate(n):
        if d >= 0:
            oh[bk[i], i] = 8.0
    c["bk_onehot"] = oh
    nm = np.zeros((4, HW_ + 128), np.float32)
    nm[:, n < 0] = NEG_BIG
    c["bk_negmask"] = nm
    return c


CONST_SHAPES = None


def build_program(dbg=False):
    nc = bass.Bass("TRN2", target_bir_lowering=False)
    consts = host_consts()

    def din(name, shape, dt=F32):
        return nc.dram_tensor(name, list(shape), dt, kind="ExternalInput").ap()

    x_d = din("x", [S, D])
    rel_bias_d = din("rel_bias", [32, NH])
    p = {}
    pshapes = {
        "norm_mix": [DEPTH, D], "w_in": [DEPTH, D, INW], "q_gain": [DEPTH, 64], "k_gain": [DEPTH, 64],
        "lambda_q1": [DEPTH, 64], "lambda_k1": [DEPTH, 64], "lambda_q2": [DEPTH, 64], "lambda_k2": [DEPTH, 64],
        "subln": [DEPTH, 128], "w_a": [DEPTH, 512, D], "lam_re": [DEPTH, G, 64], "lam_im": [DEPTH, G, 64],
        "b_re": [DEPTH, G, 64, 16], "b_im": [DEPTH, G, 64, 16], "c_re": [DEPTH, G, 16, 64], "c_im": [DEPTH, G, 16, 64],
        "d_skip": [DEPTH, 512], "log_step": [DEPTH, G], "w_glu": [DEPTH, 512, 512], "w_b": [DEPTH, 512, D],
        "w_o": [DEPTH, D, D], "norm_ffn": [DEPTH, D], "w1": [DEPTH, D, DFF], "w3": [DEPTH, D, DFF], "w2": [DEPTH, DFF, D],
    }
    for k, shp in pshapes.items():
        p[k] = din(k, shp)
    cd = {}
    for k, v in consts.items():
        cd[k] = din("c_" + k, v.shape, BF16 if v.dtype == ml_dtypes.bfloat16 else F32)
    y_d = nc.dram_tensor("y", [S, D], F32, kind="ExternalOutput").ap()
    tb_scr = nc.dram_tensor("tb_scr", [NH, HW_ + 128], BF16, kind="Internal")
    dbg_outs = {}

    es = ExitStack()
    with es:
        es.enter_context(nc.allow_non_contiguous_dma(reason="small param loads"))
        P = Prog(nc, es)

        def sb(name, shape, dt, st=es):
            return st.enter_context(nc.sbuf_tensor(name, list(shape), dt))

        xT = sb("xT", [128, 8, S], F32)
        hT = sb("hT", [128, 8, S], BF16)
        sT = sb("sT", [128, 4, S], BF16)
        ident_f = sb("ident_f", [128, 128], F32)
        ident_b = sb("ident_b", [128, 128], BF16)
        antiid_b = sb("antiid_b", [128, 128], BF16)
        ones_b = sb("ones_b", [128, 128], BF16)
        blk_b = sb("blk_b", [128, 128], BF16)
        negmask = sb("negmask", [128, 128], F32)
        sgn = sb("sgn", [128, 2], F32)
        jidx = sb("jidx", [128, 256], F32)
        jmask = sb("jmask", [128, 256], F32)
        cbias = sb("cbias", [128, NH], F32)
        epsb = sb("epsb", [128, 1], F32)
        ps = [es.enter_context(nc.psum_tensor("ps%d" % i, [128, 512], F32)) for i in range(8)]

        def PK(i):
            return ("ps", i)

        for nm_, t_ in (("ident_f", ident_f), ("ident_b", ident_b), ("antiid_b", antiid_b), ("ones_b", ones_b),
                        ("blk_b", blk_b), ("negmask", negmask), ("sgn", sgn), ("jidx", jidx), ("jmask", jmask)):
            P.dma("sp", "c_" + nm_, t_[:], cd[nm_], writes=[nm_])
        P.op("dve", lambda: nc.vector.memset(epsb[:], EPS), writes=["epsb"])
        CK = ["ident_f", "ident_b", "antiid_b", "ones_b", "blk_b", "negmask", "sgn", "jidx", "jmask"]

        with ExitStack() as st:
            rb = sb("rb", [32, NH], F32, st)
            oh = sb("oh", [32, HW_ + 128], F32, st)
            nmk = sb("nmk", [4, HW_ + 128], F32, st)
            tbs = sb("tbs", [4, HW_ + 128], BF16, st)
            P.dma("sp", "rb", rb[:], rel_bias_d, writes=["rb"])
            P.dma("sp", "oh", oh[:], cd["bk_onehot"], writes=["oh"])
            P.dma("sp", "nmk", nmk[:], cd["bk_negmask"], writes=["nmk"])
            W_ = HW_ + 128
            for c0 in range(0, W_, 448):
                n_ = min(448, W_ - c0)
                P.op("pe", lambda c0=c0, n_=n_: nc.tensor.matmul(ps[0][0:4, 0:n_], lhsT=rb[:, :], rhs=oh[:, c0:c0 + n_],
                                                                 start=True, stop=True),
                     reads=["rb", "oh"], writes=[PK(0)])
                P.op("dve", lambda c0=c0, n_=n_: nc.vector.tensor_tensor(out=tbs[:, c0:c0 + n_], in0=ps[0][0:4, 0:n_],
                                                                          in1=nmk[:, c0:c0 + n_], op=ALU.add),
                     reads=[PK(0), "nmk"], writes=["tbs"])
            P.dma("sp", "tbw", tb_scr.ap(), tbs[:], reads=["tbs"], writes=["tb_scr"])
            P.dma("sp", "cbias", cbias[:], rel_bias_d[31:32, :].partition_broadcast(128), writes=["cbias"])
            P.barrier()

        with ExitStack() as st:
            xin = [sb("xin%d" % i, [128, D], F32, st) for i in range(2)]
            for tt in range(16):
                b_ = xin[tt % 2]
                P.dma("sp", "xin%d" % (tt % 2), b_[:], x_d[tt * 128:(tt + 1) * 128, :], writes=[("xin", tt % 2)])
                for half in range(2):
                    pb = (tt * 2 + half) % 4
                    for q in range(4):
                        kt = half * 4 + q
                        P.op("pe", lambda kt=kt, q=q, pb=pb, b_=b_: nc.tensor.transpose(
                            ps[pb][:, q * 128:(q + 1) * 128], b_[:, kt * 128:(kt + 1) * 128], ident_f[:]),
                            reads=[("xin", tt % 2), "ident_f"], writes=[PK(pb)])
                    eng = "dve" if half == 0 else "act"
                    outap = xT[:, half * 4:(half + 1) * 4, tt * 128:(tt + 1) * 128]
                    inap = ps[pb][:, :].rearrange("p (q t) -> p q t", q=4)
                    if eng == "dve":
                        P.op("dve", lambda outap=outap, inap=inap: nc.vector.tensor_copy(out=outap, in_=inap),
                             reads=[PK(pb)], writes=[("xT", "all")])
                    else:
                        P.op("act", lambda outap=outap, inap=inap: nc.scalar.copy(out=outap, in_=inap),
                             reads=[PK(pb)], writes=[("xT", "all")])
            P.barrier()

        wq = {"i": 0}

        def rmsnorm_to_hT(gvec_d, st_parent, tag):
            with ExitStack() as st:
                gT = sb("gT" + tag, [128, 8], F32, st)
                P.dma("sp", "gT", gT[:], gvec_d.rearrange("(kt p) -> p kt", p=128), writes=["gT"])
                sq = [sb("nsq%d%s" % (i, tag), [128, CH], BF16, st) for i in range(2)]
                rstds = [sb("nrstd%d%s" % (i, tag), [128, CH], F32, st) for i in range(2)]
                for c in range(NCH):
                    cs = slice(c * CH, (c + 1) * CH)
                    rstd = rstds[c % 2]
                    krs = ("nrstd", c % 2)
                    for kt in range(8):
                        b_ = sq[kt % 2]
                        P.op("act", lambda kt=kt, b_=b_, cs=cs: nc.scalar.activation(out=b_[:], in_=xT[:, kt, cs], func=AF.Square),
                             reads=[("xT", "all")], writes=[("nsq", kt % 2)])
                        P.op("pe", lambda kt=kt, b_=b_, c=c: nc.tensor.matmul(ps[c % 2][:, :], lhsT=ones_b[:], rhs=b_[:],
                                                                         start=(kt == 0), stop=(kt == 7)),
                             reads=[("nsq", kt % 2), "ones_b"], writes=[PK(c % 2)])
                    P.op("act", lambda c=c, rstd=rstd: nc.scalar.activation(out=rstd[:], in_=ps[c % 2][:, :], func=AF.Ln, scale=1.0 / D, bias=epsb[:, 0:1]),
                         reads=[PK(c % 2), "epsb"], writes=[krs])
                    P.op("act", lambda: nc.scalar.activation(out=rstd[:], in_=rstd[:], func=AF.Exp, scale=-0.5),
                         reads=[krs], writes=[krs])
                    for kt in range(8):
                        P.op("dve", lambda kt=kt, cs=cs: nc.vector.scalar_tensor_tensor(
                            out=hT[:, kt, cs], in0=xT[:, kt, cs], scalar=gT[:, kt:kt + 1], in1=rstd[:],
                            op0=ALU.mult, op1=ALU.mult),
                            reads=[("xT", "all"), "gT", krs], writes=[("hT", c)])
                P.barrier()

        def load_w(slots, slot_keys, src_ap, nk, ncols, wstate):
            i = wstate["i"] % len(slots)
            wstate["i"] += 1
            dst = slots[i][:, 0:nk, 0:ncols]
            P.dma("pool", slot_keys[i], dst, src_ap.rearrange("(kt p) c -> p kt c", p=128), writes=[("w", slot_keys[i])])
            return slots[i], ("w", slot_keys[i])

        def expsmall(out_ap, x_ap, tmp_ap, keys_r, key_w, key_t):
            coef = [1.0 / math.factorial(k) for k in range(10)]
            P.op("dve", lambda: nc.vector.tensor_scalar(out=tmp_ap, in0=x_ap, scalar1=coef[9], scalar2=coef[8],
                                                        op0=ALU.mult, op1=ALU.add),
                 reads=keys_r, writes=[key_t])
            for k in range(7, -1, -1):
                P.op("dve", lambda: nc.vector.tensor_tensor(out=tmp_ap, in0=tmp_ap, in1=x_ap, op=ALU.mult),
                     reads=keys_r + [key_t], writes=[key_t])
                dst = out_ap if k == 0 else tmp_ap
                P.op("dve", lambda k=k, dst=dst: nc.vector.tensor_scalar(out=dst, in0=tmp_ap, scalar1=coef[k], scalar2=None,
                                                                          op0=ALU.add),
                     reads=[key_t], writes=[key_w if k == 0 else key_t])

        C1 = 6.28125
        C2 = TWO_PI - 6.28125

        def sin_pos(out_ap, x_ap, tmp_ap, tmp2_ap, mult, add, keys_r, key_w, key_t, key_t2, do_sin=True):
            ki = tmp2_ap.bitcast(mybir.dt.int32)
            P.op("dve", lambda: nc.vector.tensor_scalar(out=ki, in0=x_ap, scalar1=float(mult) / TWO_PI, scalar2=float(add) / TWO_PI,
                                                        op0=ALU.mult, op1=ALU.add),
                 reads=keys_r, writes=[key_t2])
            P.op("dve", lambda: nc.vector.tensor_copy(out=tmp_ap, in_=ki), reads=[key_t2], writes=[key_t])
            P.op("dve", lambda: nc.vector.tensor_scalar(out=tmp2_ap, in0=x_ap, scalar1=float(mult), scalar2=float(add),
                                                        op0=ALU.mult, op1=ALU.add),
                 reads=keys_r + [key_t], writes=[key_t2])
            P.op("dve", lambda: nc.vector.scalar_tensor_tensor(out=tmp2_ap, in0=tmp_ap, scalar=-C1, in1=tmp2_ap, op0=ALU.mult, op1=ALU.add),
                 reads=[key_t, key_t2], writes=[key_t2])
            P.op("dve", lambda: nc.vector.scalar_tensor_tensor(out=tmp2_ap, in0=tmp_ap, scalar=-C2, in1=tmp2_ap, op0=ALU.mult, op1=ALU.add),
                 reads=[key_t, key_t2], writes=[key_t2])
            P.op("dve", lambda: nc.vector.tensor_scalar(out=tmp_ap, in0=tmp2_ap, scalar1=math.pi, scalar2=-TWO_PI, op0=ALU.is_gt, op1=ALU.mult),
                 reads=[key_t2], writes=[key_t])
            P.op("dve", lambda: nc.vector.tensor_tensor(out=tmp2_ap, in0=tmp2_ap, in1=tmp_ap, op=ALU.add), reads=[key_t, key_t2], writes=[key_t2])
            P.op("dve", lambda: nc.vector.tensor_scalar(out=tmp_ap, in0=tmp2_ap, scalar1=-math.pi, scalar2=TWO_PI, op0=ALU.is_lt, op1=ALU.mult),
                 reads=[key_t2], writes=[key_t])
            P.op("dve", lambda: nc.vector.tensor_tensor(out=tmp2_ap, in0=tmp2_ap, in1=tmp_ap, op=ALU.add), reads=[key_t, key_t2], writes=[key_t2])
            P.op("dve", lambda: nc.vector.tensor_scalar(out=tmp2_ap, in0=tmp2_ap, scalar1=-math.pi, scalar2=math.pi, op0=ALU.max, op1=ALU.min),
                 reads=[key_t2], writes=[key_t2])
            if do_sin:
                P.op("act", lambda: nc.scalar.activation(out=out_ap, in_=tmp2_ap, func=AF.Sin), reads=[key_t2], writes=[key_w])
            else:
                P.op("dve", lambda: nc.vector.tensor_copy(out=out_ap, in_=tmp2_ap), reads=[key_t2], writes=[key_w])

        halfpi = sb("halfpi", [128, 1], F32)
        P.op("dve", lambda: nc.vector.memset(halfpi[:], math.pi / 2), writes=["halfpi"])

        def sincos_pos(sin_ap, cos_ap, x_ap, tA, tB, mult, add, keys_r, key_s, key_c, key_a, key_b):
            ki = tA.bitcast(mybir.dt.int32)
            P.op("dve", lambda: nc.vector.tensor_scalar(out=ki, in0=x_ap, scalar1=float(mult) / TWO_PI, scalar2=float(add) / TWO_PI,
                                                        op0=ALU.mult, op1=ALU.add), reads=keys_r, writes=[key_a])
            P.op("dve", lambda: nc.vector.tensor_copy(out=tB, in_=ki), reads=[key_a], writes=[key_b])
            if mult == 1.0 and add == 0.0:
                P.op("dve", lambda: nc.vector.scalar_tensor_tensor(out=tA, in0=tB, scalar=-C1, in1=x_ap, op0=ALU.mult, op1=ALU.add),
                     reads=keys_r + [key_b], writes=[key_a])
            else:
                P.op("dve", lambda: nc.vector.tensor_scalar(out=tA, in0=x_ap, scalar1=float(mult), scalar2=float(add),
                                                            op0=ALU.mult, op1=ALU.add), reads=keys_r + [key_b], writes=[key_a])
                P.op("dve", lambda: nc.vector.scalar_tensor_tensor(out=tA, in0=tB, scalar=-C1, in1=tA, op0=ALU.mult, op1=ALU.add),
                     reads=[key_a, key_b], writes=[key_a])
            P.op("dve", lambda: nc.vector.scalar_tensor_tensor(out=tA, in0=tB, scalar=-C2, in1=tA, op0=ALU.mult, op1=ALU.add),
                 reads=[key_a, key_b], writes=[key_a])
            P.op("dve", lambda: nc.vector.tensor_scalar(out=tB, in0=tA, scalar1=math.pi, scalar2=-TWO_PI, op0=ALU.is_gt, op1=ALU.mult),
                 reads=[key_a], writes=[key_b])
            P.op("dve", lambda: nc.vector.tensor_tensor(out=tA, in0=tA, in1=tB, op=ALU.add), reads=[key_a, key_b], writes=[key_a])
            P.op("dve", lambda: nc.vector.tensor_scalar(out=tA, in0=tA, scalar1=-math.pi, scalar2=math.pi, op0=ALU.max, op1=ALU.min),
                 reads=[key_a], writes=[key_a])
            P.op("act", lambda: nc.scalar.activation(out=sin_ap, in_=tA, func=AF.Sin), reads=[key_a], writes=[key_s])
            P.op("dve", lambda: nc.vector.tensor_scalar(out=tB, in0=tA, scalar1=-1.0, scalar2=None, op0=ALU.mult), reads=[key_a], writes=[key_b])
            P.op("dve", lambda: nc.vector.tensor_tensor(out=tB, in0=tB, in1=tA, op=ALU.max), reads=[key_a, key_b], writes=[key_b])
            P.op("act", lambda: nc.scalar.activation(out=cos_ap, in_=tB, func=AF.Sin, scale=-1.0, bias=halfpi[:, 0:1]),
                 reads=[key_b, "halfpi"], writes=[key_c])

        DBG = set(os.environ.get("KDBG", "").split(",")) - {""}

        def dump(name, ap, shape, dt=F32):
            if name not in DBG:
                return
            t = nc.dram_tensor("dbg_" + name, list(shape), dt, kind="ExternalOutput").ap()
            P.barrier()
            P.dma("sp", "dbg_" + name, t, ap)
            P.barrier()

        def checkpoint(n, stacks=()):
            if n > KSTOP:
                P.barrier()
                for s_ in stacks:
                    s_.close()
                raise _Stop()

        def _layer(l):
            lam_init = 0.8 - 0.6 * math.exp(-0.3 * l)
            L = "L%d" % l
            w_in_l = p["w_in"][l]

            checkpoint(10 * l + 0)
            rmsnorm_to_hT(p["norm_mix"][l], es, "a" + L)

            if l == 0:
                dump("hT0", hT[:], [128, 8, S], BF16)
            checkpoint(10 * l + 1)
            with ExitStack() as st:
                wus = [sb("wu%d%s" % (i, L), [128, 8, 128], BF16, st) for i in range(1)]
                lr2 = sb("lr2" + L, [128, G], F32, st)
                li2 = sb("li2" + L, [128, G], F32, st)
                stp = sb("stp" + L, [128, G], F32, st)
                Cs = sb("Cs" + L, [128, G, 16], F32, st)
                Csw = sb("Csw" + L, [128, G, 16], F32, st)
                with ExitStack() as stl:
                    lraw = sb("lraw" + L, [32, 2, 128], F32, stl)
                    for half in range(2):
                        P.dma("sp", "lraw", lraw[:, 0, half * 64:(half + 1) * 64], p["lam_re"][l], writes=["lraw"])
                        P.dma("sp", "lraw", lraw[:, 1, half * 64:(half + 1) * 64], p["lam_im"][l], writes=["lraw"])
                    for i_, dst_ in enumerate((lr2, li2)):
                        P.op("pe", lambda i_=i_: nc.tensor.transpose(ps[0][:, i_ * 32:(i_ + 1) * 32], lraw[:, i_, :], ident_f[0:32, 0:32]),
                             reads=["lraw", "ident_f"], writes=[PK(0)])
                    P.op("dve", lambda: nc.vector.tensor_copy(out=lr2[:], in_=ps[0][:, 0:32]), reads=[PK(0)], writes=["lr2"])
                    P.op("dve", lambda: nc.vector.tensor_copy(out=li2[:], in_=ps[0][:, 32:64]), reads=[PK(0)], writes=["li2"])
                    craw = sb("craw" + L, [128, 4, 2, 128], F32, stl)
                    c_re_v = p["c_re"][l].rearrange("g c p -> (g c) p")
                    c_im_v = p["c_im"][l].rearrange("g c p -> (g c) p")
                    for rt in range(4):
                        rs_ = slice(rt * 128, (rt + 1) * 128)
                        P.dma("sp", "craw", craw[:, rt, 0, 0:64], c_re_v[rs_, :], writes=["craw"])
                        P.dma("sp", "craw", craw[:, rt, 0, 64:128], c_im_v[rs_, :], writes=["craw"])
                        P.dma("sp", "craw", craw[:, rt, 1, 0:64], c_im_v[rs_, :], writes=["craw"])
                        P.dma("sp", "craw", craw[:, rt, 1, 64:128], c_re_v[rs_, :], writes=["craw"])
                    for w_, dst_, dk in ((0, Cs, "Cs"), (1, Csw, "Csw")):
                        for rt in range(4):
                            P.op("pe", lambda rt=rt, w_=w_: nc.tensor.transpose(ps[1 + w_][:, rt * 128:(rt + 1) * 128], craw[:, rt, w_, :], ident_f[:]),
                                 reads=["craw", "ident_f"], writes=[PK(1 + w_)])
                        P.op("dve", lambda w_=w_, dst_=dst_: nc.vector.tensor_copy(out=dst_[:].rearrange("p g c -> p (g c)"), in_=ps[1 + w_][:, :]),
                             reads=[PK(1 + w_)], writes=[dk])
                    P.barrier()
                checkpoint(10 * l + 1.1, [st])
                P.dma("sp", "stp", stp[:], p["log_step"][l:l + 1, :].partition_broadcast(128), writes=["stp"])
                d16 = sb("d16" + L, [128, G], F32, st)
                for i in range(8):
                    P.dma("sp", "d16", d16[i * 16:(i + 1) * 16, :], p["d_skip"][l].rearrange("(g c) -> c g", c=16), writes=["d16"])

                ls = sb("ls" + L, [128, G], F32, st)
                th = sb("th" + L, [128, G], F32, st)
                t0 = sb("t0" + L, [128, G], F32, st)
                t1 = sb("t1" + L, [128, G], F32, st)
                t2 = sb("t2" + L, [128, G], F32, st)
                t3 = sb("t3" + L, [128, G], F32, st)
                t4 = sb("t4" + L, [128, G], F32, st)
                rho1 = sb("rho1" + L, [128, G], F32, st)
                rho8 = sb("rho8" + L, [128, G], F32, st)
                phi = sb("phi" + L, [128, G], F32, st)
                cn = sb("cn" + L, [128, G], F32, st)
                sn = sb("sn" + L, [128, G], F32, st)
                nr = sb("nr" + L, [128, G], F32, st)
                ni = sb("ni" + L, [128, G], F32, st)
                den = sb("den" + L, [128, G], F32, st)
                fre = sb("fre" + L, [128, G], F32, st)
                fim = sb("fim" + L, [128, G], F32, st)
                FB = sb("FB" + L, [128, G], F32, st)
                FD = sb("FD" + L, [128, G], F32, st)
                Bs = sb("Bs" + L, [128, G, 16], F32, st)
                Bsw = sb("Bsw" + L, [128, G, 16], F32, st)
                rW = sb("rW" + L, [128, G, 8], F32, st)
                rV = sb("rV" + L, [128, G, 8], F32, st)
                rVn = sb("rVn" + L, [128, G, 8], F32, st)
                csn = sb("csn" + L, [128, G, 8], F32, st)
                snn = sb("snn" + L, [128, G, 8], F32, st)

                st2 = ExitStack()
                braw = sb("braw" + L, [128, G, 16], F32, st2)
                brsw = sb("brsw" + L, [128, G, 16], F32, st2)
                P.dma("sp", "braw", braw[0:64], p["b_re"][l].rearrange("g p c -> p g c"), writes=["braw"])
                P.dma("sp", "braw", braw[64:128], p["b_im"][l].rearrange("g p c -> p g c"), writes=["braw"])
                P.dma("sp", "brsw", brsw[0:64], p["b_im"][l].rearrange("g p c -> p g c"), writes=["brsw"])
                P.dma("sp", "brsw", brsw[64:128], p["b_re"][l].rearrange("g p c -> p g c"), writes=["brsw"])
                tB = sb("tB" + L, [128, G, 16], F32, st2)
                def V(fn, r, w):
                    return P.op("dve", fn, reads=r, writes=w)

                def Gp(fn, r, w):
                    return P.op("pool", fn, reads=r, writes=w)

                checkpoint(10 * l + 1.2, [st2, st])
                P.op("act", lambda: nc.scalar.activation(out=stp[:], in_=stp[:], func=AF.Exp), reads=["stp"], writes=["stp"])
                V(lambda: nc.vector.tensor_tensor(out=ls[:], in0=lr2[:], in1=stp[:], op=ALU.mult), ["lr2", "stp"], ["ls"])
                V(lambda: nc.vector.tensor_tensor(out=th[:], in0=li2[:], in1=stp[:], op=ALU.mult), ["li2", "stp"], ["th"])
                expsmall(rho1[:], ls[:], t0[:], ["ls"], "rho1", "t0")
                V(lambda: nc.vector.tensor_scalar(out=t1[:], in0=ls[:], scalar1=8.0, scalar2=None, op0=ALU.mult), ["ls"], ["t1"])
                expsmall(rho8[:], t1[:], t0[:], ["t1"], "rho8", "t0")
                sincos_pos(sn[:], cn[:], th[:], t2[:], t4[:], 1.0, TWO_PI, ["th"], "sn", "cn", "t2", "t4")
                sin_pos(phi[:], th[:], t0[:], t3[:], 1.0, TWO_PI, ["th"], "phi", "t0", "t3", do_sin=False)
                sin_pos(phi[:], phi[:], t0[:], t3[:], 8.0, 5 * TWO_PI, ["phi"], "phi", "t0", "t3", do_sin=False)
                V(lambda: nc.vector.tensor_scalar(out=t0[:], in0=phi[:], scalar1=0.0, scalar2=TWO_PI, op0=ALU.is_lt, op1=ALU.mult), ["phi"], ["t0"])
                V(lambda: nc.vector.tensor_tensor(out=phi[:], in0=phi[:], in1=t0[:], op=ALU.add), ["phi", "t0"], ["phi"])
                checkpoint(10 * l + 1.3, [st2, st])
                V(lambda: nc.vector.tensor_tensor(out=nr[:], in0=rho1[:], in1=cn[:], op=ALU.mult), ["rho1", "cn"], ["nr"])
                V(lambda: nc.vector.tensor_scalar(out=nr[:], in0=nr[:], scalar1=-1.0, scalar2=None, op0=ALU.add),
                  ["nr"], ["nr"])
                V(lambda: nc.vector.tensor_tensor(out=ni[:], in0=rho1[:], in1=sn[:], op=ALU.mult), ["rho1", "sn"], ["ni"])
                V(lambda: nc.vector.tensor_tensor(out=den[:], in0=lr2[:], in1=lr2[:], op=ALU.mult), ["lr2"], ["den"])
                V(lambda: nc.vector.tensor_tensor(out=t1[:], in0=li2[:], in1=li2[:], op=ALU.mult), ["li2"], ["t1"])
                V(lambda: nc.vector.tensor_tensor(out=den[:], in0=den[:], in1=t1[:], op=ALU.add), ["den", "t1"], ["den"])
                V(lambda: nc.vector.reciprocal(out=den[:], in_=den[:]), ["den"], ["den"])
                V(lambda: nc.vector.tensor_tensor(out=fre[:], in0=nr[:], in1=lr2[:], op=ALU.mult), ["nr", "lr2"], ["fre"])
                V(lambda: nc.vector.tensor_tensor(out=t1[:], in0=ni[:], in1=li2[:], op=ALU.mult), ["ni", "li2"], ["t1"])
                V(lambda: nc.vector.tensor_tensor(out=fre[:], in0=fre[:], in1=t1[:], op=ALU.add), ["fre", "t1"], ["fre"])
                V(lambda: nc.vector.tensor_tensor(out=fre[:], in0=fre[:], in1=den[:], op=ALU.mult), ["fre", "den"], ["fre"])
                V(lambda: nc.vector.tensor_tensor(out=fim[:], in0=ni[:], in1=lr2[:], op=ALU.mult), ["ni", "lr2"], ["fim"])
                V(lambda: nc.vector.tensor_tensor(out=t1[:], in0=nr[:], in1=li2[:], op=ALU.mult), ["nr", "li2"], ["t1"])
                V(lambda: nc.vector.tensor_tensor(out=fim[:], in0=fim[:], in1=t1[:], op=ALU.subtract), ["fim", "t1"], ["fim"])
                V(lambda: nc.vector.tensor_tensor(out=fim[:], in0=fim[:], in1=den[:], op=ALU.mult), ["fim", "den"], ["fim"])
                V(lambda: nc.vector.tensor_scalar(out=FB[:], in0=fim[:], scalar1=sgn[:, 0:1], scalar2=None, op0=ALU.mult),
                  ["fim", "sgn"], ["FB"])
                V(lambda: nc.vector.tensor_scalar(out=FD[:], in0=fre[:], scalar1=sgn[:, 1:2], scalar2=None, op0=ALU.mult),
                  ["fre", "sgn"], ["FD"])

                checkpoint(10 * l + 1.4, [st2, st])

                def bc16(t):
                    return t[:, :].unsqueeze(2).broadcast_to([128, G, 16])

                V(lambda: nc.vector.tensor_tensor(out=Bs[:], in0=braw[:], in1=bc16(fre), op=ALU.mult), ["braw", "fre"], ["Bs"])
                V(lambda: nc.vector.tensor_tensor(out=tB[:], in0=brsw[:], in1=bc16(FB), op=ALU.mult), ["brsw", "FB"], ["tB"])
                V(lambda: nc.vector.tensor_tensor(out=Bs[:], in0=Bs[:], in1=tB[:], op=ALU.add), ["Bs", "tB"], ["Bs"])
                V(lambda: nc.vector.tensor_tensor(out=Bsw[:], in0=braw[:], in1=bc16(fim), op=ALU.mult), ["braw", "fim"], ["Bsw"])
                V(lambda: nc.vector.tensor_tensor(out=tB[:], in0=brsw[:], in1=bc16(FD), op=ALU.mult), ["brsw", "FD"], ["tB"])
                V(lambda: nc.vector.tensor_tensor(out=Bsw[:], in0=Bsw[:], in1=tB[:], op=ALU.add), ["Bsw", "tB"], ["Bsw"])
                checkpoint(10 * l + 1.5, [st2, st])
                P.barrier()
                st2.close()
                with ExitStack() as stv:
                    sth = sb("sth" + L, [128, G, 8], F32, stv)
                    tvA = sb("tvA" + L, [128, G, 8], F32, stv)
                    tvB = sb("tvB" + L, [128, G, 8], F32, stv)
                    sidx = jidx[:, 0:8].unsqueeze(1).broadcast_to([128, G, 8])
                    V(lambda: nc.vector.tensor_tensor(out=sth[:], in0=ls[:, :].unsqueeze(2).broadcast_to([128, G, 8]), in1=sidx, op=ALU.mult),
                      ["ls", "jidx"], ["sth"])
                    fl3 = lambda t: t[:].rearrange("p g s -> p (g s)")
                    P.op("act", lambda: nc.scalar.activation(out=fl3(rW), in_=fl3(sth), func=AF.Exp, scale=-1.0), reads=["sth"], writes=["rW"])
                    P.op("act", lambda: nc.scalar.activation(out=fl3(rV), in_=fl3(sth), func=AF.Exp, scale=1.0), reads=["sth"], writes=["rV"])
                    V(lambda: nc.vector.tensor_tensor(out=sth[:], in0=th[:, :].unsqueeze(2).broadcast_to([128, G, 8]), in1=sidx, op=ALU.mult),
                      ["th", "jidx"], ["sth"])
                    sincos_pos(fl3(snn), fl3(csn), fl3(sth), fl3(tvA), fl3(tvB), 1.0, TWO_PI, ["sth"], "snn", "csn", "tvA", "tvB")
                    P.barrier()

                V(lambda: nc.vector.tensor_scalar(out=Cs[:], in0=Cs[:], scalar1=sgn[:, 1:2], scalar2=None, op0=ALU.mult), ["Cs", "sgn"], ["Cs"])
                V(lambda: nc.vector.tensor_tensor(out=rVn[:], in0=csn[:], in1=rV[:], op=ALU.mult), ["csn", "rV"], ["rVn"])
                V(lambda: nc.vector.tensor_tensor(out=rV[:], in0=snn[:], in1=rV[:], op=ALU.mult), ["snn", "rV"], ["rV"])
                V(lambda: nc.vector.tensor_tensor(out=csn[:], in0=csn[:], in1=rW[:], op=ALU.mult), ["csn", "rW"], ["csn"])
                V(lambda: nc.vector.tensor_tensor(out=snn[:], in0=snn[:], in1=rW[:], op=ALU.mult), ["snn", "rW"], ["snn"])
                checkpoint(10 * l + 2, [st])
                U8b = sb("U8b" + L, [128, 2, 8, 8, 16], F32, st)
                Ub = sb("Ub" + L, [128, 8, 256], F32, st)
                S8b = sb("S8b" + L, [128, 2, 8, 128], BF16, st)
                WnS = [sb("Wn%d%s" % (i, L), [128, 4, 8, 16], F32, st) for i in range(2)]
                WswnS = [sb("Wswn%d%s" % (i, L), [128, 4, 8, 16], F32, st) for i in range(2)]
                VnS = [sb("Vn%d%s" % (i, L), [128, 4, 8, 16], F32, st) for i in range(2)]
                VswnS = [sb("Vswn%d%s" % (i, L), [128, 4, 8, 16], F32, st) for i in range(2)]
                tP = sb("tP" + L, [128, 4, 8, 16], F32, st)
                WnT = sb("WnT" + L, [128, 4, 128], F32, st)
                WswnT = sb("WswnT" + L, [128, 4, 128], F32, st)
                Tp = sb("Tp" + L, [128, 4, 128], F32, st)
                COSn = sb("COSn" + L, [128, 4, 256], F32, st)
                SINn = sb("SINn" + L, [128, 4, 256], F32, st)
                RHO = sb("RHO" + L, [128, 4, 256], F32, st)
                vin = sb("vin" + L, [128, 4, 256], F32, st)
                vv = sb("vv" + L, [128, 4, 256], F32, st)
                cv = sb("cv" + L, [128, 4, 256], F32, st)
                Sg = sb("Sg" + L, [128, 8, 256], BF16, st)

                for b in range(4):
                    wu = wus[0]
                    P.dma("pool", "wu0", wu[:, :, :],
                          w_in_l[:, 1536 + b * 128:1536 + (b + 1) * 128].rearrange("(kt p) c -> p kt c", p=128), writes=[("wu", 0)])
                    for half in range(2):
                        pb = half
                        for q in range(8):
                            tk = half * 8 + q
                            jt, s_ = tk // 8, tk % 8
                            t_start = jt * 1024 + s_
                            for kt in range(8):
                                P.op("pe", lambda kt=kt, q=q, pb=pb, t_start=t_start: nc.tensor.matmul(
                                    ps[pb][:, q * 64:(q + 1) * 64],
                                    lhsT=hT[:, kt, t_start:t_start + 1017:8], rhs=wu[:, kt, 0:64],
                                    start=(kt == 0), stop=(kt == 7)),
                                    reads=[("hT", 0), ("hT", 1), ("hT", 2), ("hT", 3), ("wu", 0)], writes=[PK(pb)], signal=(kt == 7))
                    for half in range(2):
                        pb = 2 + half
                        for q in range(8):
                            tk = half * 8 + q
                            jt, s_ = tk // 8, tk % 8
                            t_start = jt * 1024 + s_
                            for kt in range(8):
                                P.op("pe", lambda kt=kt, q=q, pb=pb, t_start=t_start: nc.tensor.matmul(
                                    ps[pb][:, q * 64:(q + 1) * 64],
                                    lhsT=hT[:, kt, t_start:t_start + 1017:8], rhs=wu[:, kt, 64:128],
                                    start=(kt == 0), stop=(kt == 7)),
                                    reads=[("hT", 0), ("hT", 1), ("hT", 2), ("hT", 3), ("wu", 0)], writes=[PK(pb)], signal=(kt == 7))
                    for half in range(2):
                        for cc in range(2):
                            pb = cc * 2 + half
                            outap = U8b[:, half, cc * 4:(cc + 1) * 4, :, :].rearrange("p g s c -> p s g c")
                            inap = ps[pb][:, :].rearrange("p (s g c) -> p s g c", s=8, g=4)
                            if cc == 0:
                                P.op("dve", lambda outap=outap, inap=inap: nc.vector.tensor_copy(out=outap, in_=inap),
                                     reads=[PK(pb)], writes=["U8b"])
                            else:
                                P.op("act", lambda outap=outap, inap=inap: nc.scalar.copy(out=outap, in_=inap),
                                     reads=[PK(pb)], writes=["U8b"])
                    for gl in range(8):
                        pb = 4 + (gl // 2) % 2
                        for jt in range(2):
                            col = ((gl % 2) * 2 + jt) * 128
                            P.op("pe", lambda gl=gl, jt=jt, pb=pb, col=col: nc.tensor.transpose(
                                ps[pb][:, col:col + 128], U8b[:, jt, gl, :, :].rearrange("p s c -> p (s c)"), ident_f[:]),
                                reads=["U8b", "ident_f"], writes=[PK(pb)])
                        if gl % 2 == 1:
                            outap = Ub[:, gl - 1:gl + 1, :]
                            inap = ps[pb][:, :].rearrange("p (g j) -> p g j", g=2)
                            P.op("dve", lambda outap=outap, inap=inap: nc.vector.tensor_copy(out=outap, in_=inap),
                                 reads=[PK(pb)], writes=[("Ub", gl // 2)])
                    for hb in range(2):
                        g0 = b * 8 + hb * 4
                        si = (b * 2 + hb) % 2
                        Wn, Wswn, Vn, Vswn = WnS[si], WswnS[si], VnS[si], VswnS[si]
                        kWn, kWswn, kVn, kVswn = "Wn%d" % si, "Wswn%d" % si, "Vn%d" % si, "Vswn%d" % si

                        def bcs(t):
                            return t[:, g0:g0 + 4, :].unsqueeze(3).broadcast_to([128, 4, 8, 16])

                        def bcg(t):
                            return t[:, g0:g0 + 4, :].unsqueeze(2).broadcast_to([128, 4, 8, 16])

                        Gp(lambda: nc.gpsimd.tensor_tensor(out=Wn[:], in0=bcs(csn), in1=bcg(Bs), op=ALU.mult), ["csn", "Bs"], [kWn])
                        Gp(lambda: nc.gpsimd.tensor_tensor(out=tP[:], in0=bcs(snn), in1=bcg(Bsw), op=ALU.mult), ["snn", "Bsw"], ["tP"])
                        Gp(lambda: nc.gpsimd.tensor_tensor(out=Wn[:], in0=Wn[:], in1=tP[:], op=ALU.add), [kWn, "tP"], [kWn])
                        Gp(lambda: nc.gpsimd.tensor_tensor(out=Wswn[:], in0=bcs(csn), in1=bcg(Bsw), op=ALU.mult), ["csn", "Bsw"], [kWswn])
                        Gp(lambda: nc.gpsimd.tensor_tensor(out=tP[:], in0=bcs(snn), in1=bcg(Bs), op=ALU.mult), ["snn", "Bs"], ["tP"])
                        Gp(lambda: nc.gpsimd.tensor_tensor(out=Wswn[:], in0=Wswn[:], in1=tP[:], op=ALU.subtract), [kWswn, "tP"], [kWswn])
                        Gp(lambda: nc.gpsimd.tensor_tensor(out=Vn[:], in0=bcs(rVn), in1=bcg(Cs), op=ALU.mult), ["rVn", "Cs"], [kVn])
                        Gp(lambda: nc.gpsimd.tensor_tensor(out=tP[:], in0=bcs(rV), in1=bcg(Csw), op=ALU.mult), ["rV", "Csw"], ["tP"])
                        Gp(lambda: nc.gpsimd.tensor_tensor(out=Vn[:], in0=Vn[:], in1=tP[:], op=ALU.subtract), [kVn, "tP"], [kVn])
                        Gp(lambda: nc.gpsimd.tensor_tensor(out=Vswn[:], in0=bcs(rV), in1=bcg(Cs), op=ALU.mult), ["rV", "Cs"], [kVswn])
                        Gp(lambda: nc.gpsimd.tensor_tensor(out=tP[:], in0=bcs(rVn), in1=bcg(Csw), op=ALU.mult), ["rVn", "Csw"], ["tP"])
                        Gp(lambda: nc.gpsimd.tensor_tensor(out=Vswn[:], in0=Vswn[:], in1=tP[:], op=ALU.add), [kVswn, "tP"], [kVswn])
                        Gp(lambda: nc.gpsimd.tensor_tensor(out=tP[:].rearrange("p g s c -> p g (s c)"), in0=ident_f[:, :].unsqueeze(1).broadcast_to([128, 4, 128]),
                                                           in1=d16[:, g0:g0 + 4].unsqueeze(2).broadcast_to([128, 4, 128]), op=ALU.mult),
                           ["ident_f", "d16"], ["tP"])

                        def flat(t, g_):
                            return t[:, g_, :, :].rearrange("p s c -> p (s c)")

                        for g_ in range(4):
                            P.op("pe", lambda g_=g_: nc.tensor.transpose(ps[6][:, g_ * 128:(g_ + 1) * 128], flat(Wn, g_), ident_f[:]),
                                 reads=[kWn, "ident_f"], writes=[PK(6)])
                        V(lambda: nc.vector.tensor_copy(out=WnT[:], in_=ps[6][:, :].rearrange("p (g m) -> p g m", g=4)), [PK(6)], ["WnT"])
                        for g_ in range(4):
                            P.op("pe", lambda g_=g_: nc.tensor.transpose(ps[7][:, g_ * 128:(g_ + 1) * 128], flat(Wswn, g_), ident_f[:]),
                                 reads=[kWswn, "ident_f"], writes=[PK(7)])
                        P.op("act", lambda: nc.scalar.copy(out=WswnT[:], in_=ps[7][:, :].rearrange("p (g m) -> p g m", g=4)),
                             reads=[PK(7)], writes=["WswnT"])
                        for g_ in range(4):
                            P.op("pe", lambda g_=g_: nc.tensor.matmul(ps[6][:, g_ * 128:(g_ + 1) * 128], lhsT=flat(Wn, g_), rhs=flat(Vn, g_),
                                                                     start=True, stop=True),
                                 reads=[kWn, kVn], writes=[PK(6)])
                        V(lambda: nc.vector.tensor_tensor(out=Tp[:], in0=ps[6][:, :].rearrange("p (g m) -> p g m", g=4),
                                                          in1=negmask[:, :].unsqueeze(1).broadcast_to([128, 4, 128]), op=ALU.mult),
                          [PK(6), "negmask"], ["Tp"])
                        V(lambda: nc.vector.tensor_tensor(out=Tp[:], in0=Tp[:], in1=tP[:].rearrange("p g s c -> p g (s c)"), op=ALU.add), ["Tp", "tP"], ["Tp"])
                        for g_ in range(4):
                            gl = hb * 4 + g_
                            P.op("pe", lambda g_=g_, gl=gl: nc.tensor.matmul(ps[g_ // 2][:, (g_ % 2) * 256:(g_ % 2 + 1) * 256], lhsT=WnT[:, g_, :],
                                                                             rhs=Ub[:, gl, :], start=True, stop=True),
                                 reads=["WnT", ("Ub", gl // 2)], writes=[PK(g_ // 2)])
                            P.op("pe", lambda g_=g_, gl=gl: nc.tensor.matmul(ps[2 + g_ // 2][:, (g_ % 2) * 256:(g_ % 2 + 1) * 256], lhsT=WswnT[:, g_, :],
                                                                             rhs=Ub[:, gl, :], start=True, stop=True),
                                 reads=["WswnT", ("Ub", gl // 2)], writes=[PK(2 + g_ // 2)])
                        V(lambda: nc.vector.tensor_tensor(out=cv[:], in0=phi[:, g0:g0 + 4].unsqueeze(2).broadcast_to([128, 4, 256]),
                                                          in1=jidx[:, :].unsqueeze(1).broadcast_to([128, 4, 256]), op=ALU.mult),
                          ["phi", "jidx"], ["cv"])
                        fl = lambda t: t[:].rearrange("p g j -> p (g j)")
                        sincos_pos(fl(SINn), fl(COSn), fl(cv), fl(vin), fl(vv), 1.0, 0.0, ["cv"], "SINn", "COSn", "vin", "vv")
                        V(lambda: nc.vector.tensor_tensor(out=RHO[:], in0=rho8[:, g0:g0 + 4].unsqueeze(2).broadcast_to([128, 4, 256]),
                                                          in1=jmask[:, :].unsqueeze(1).broadcast_to([128, 4, 256]), op=ALU.mult),
                          ["rho8", "jmask"], ["RHO"])
                        for hh in range(2):
                            V(lambda hh=hh: nc.vector.tensor_tensor(out=vin[:, hh * 2:hh * 2 + 2, :], in0=ps[hh][:, :].rearrange("p (g j) -> p g j", g=2),
                                                                    in1=COSn[:, hh * 2:hh * 2 + 2, :], op=ALU.mult),
                              [PK(hh), "COSn", "SINn"], ["vin"])
                            V(lambda hh=hh: nc.vector.tensor_tensor(out=cv[:, hh * 2:hh * 2 + 2, :], in0=ps[2 + hh][:, :].rearrange("p (g j) -> p g j", g=2),
                                                                    in1=SINn[:, hh * 2:hh * 2 + 2, :], op=ALU.mult),
                              [PK(2 + hh), "SINn"], ["cv"])
                        V(lambda: nc.vector.tensor_tensor(out=vin[:], in0=vin[:], in1=cv[:], op=ALU.add), ["vin", "cv"], ["vin"])
                        V(lambda: nc.vector.tensor_tensor_scan(out=vv[:].rearrange("p g j -> p (g j)"), data0=RHO[:].rearrange("p g j -> p (g j)"),
                                                               data1=vin[:].rearrange("p g j -> p (g j)"), initial=0.0,
                                                               op0=ALU.mult, op1=ALU.add),
                          ["RHO", "vin"], ["vv"])
                        V(lambda: nc.vector.tensor_tensor(out=cv[:], in0=COSn[:], in1=vv[:], op=ALU.mult), ["COSn", "vv"], ["cv"])
                        V(lambda: nc.vector.scalar_tensor_tensor(out=vin[:], in0=SINn[:], scalar=-1.0, in1=vv[:], op0=ALU.mult, op1=ALU.mult), ["SINn", "vv"], ["vin"])
                        for g_ in range(4):
                            gl = hb * 4 + g_
                            o_ = ps[4 + g_ // 2][:, (g_ % 2) * 256:(g_ % 2 + 1) * 256]
                            P.op("pe", lambda g_=g_, gl=gl, o_=o_: nc.tensor.matmul(o_, lhsT=Tp[:, g_, :], rhs=Ub[:, gl, :], start=True, stop=False),
                                 reads=["Tp", ("Ub", gl // 2)], writes=[PK(4 + g_ // 2)], signal=False)
                            P.op("pe", lambda g_=g_, o_=o_: nc.tensor.matmul(o_, lhsT=flat(Vn, g_), rhs=cv[:, g_, :], start=False, stop=False),
                                 reads=[kVn, "cv"], writes=[PK(4 + g_ // 2)], signal=False)
                            P.op("pe", lambda g_=g_, o_=o_: nc.tensor.matmul(o_, lhsT=flat(Vswn, g_), rhs=vin[:, g_, :], start=False, stop=True),
                                 reads=[kVswn, "vin"], writes=[PK(4 + g_ // 2)])
                        for hh in range(2):
                            P.op("act", lambda hh=hh: nc.scalar.activation(out=Sg[:, hb * 4 + hh * 2:hb * 4 + hh * 2 + 2, :],
                                                                           in_=ps[4 + hh][:, :].rearrange("p (g j) -> p g j", g=2),
                                                                           func=AF.Gelu_apprx_tanh),
                                 reads=[PK(4 + hh)], writes=["Sg"])
                    for jt in range(2):
                        pb = 6 + jt
                        psb = ps[pb][:, :].bitcast(BF16)
                        for gl in range(8):
                            P.op("pe", lambda gl=gl, jt=jt, psb=psb: nc.tensor.transpose(psb[:, gl * 128:(gl + 1) * 128],
                                                                                         Sg[:, gl, jt * 128:(jt + 1) * 128], ident_b[:]),
                                 reads=["Sg", "ident_b"], writes=[PK(pb)])
                        outap = S8b[:, jt, :, :].rearrange("p i (g c) -> p g i c", g=8)
                        inap = psb.rearrange("p (g i c) -> p g i c", g=8, i=8)
                        P.op("dve", lambda outap=outap, inap=inap: nc.vector.tensor_copy(out=outap, in_=inap), reads=[PK(pb)], writes=["S8b"])
                    for jt in range(2):
                        pb = 6 + jt
                        psb = ps[pb][:, :].bitcast(BF16)
                        for i in range(8):
                            P.op("pe", lambda i=i, jt=jt, psb=psb: nc.tensor.transpose(psb[:, i * 128:(i + 1) * 128], S8b[:, jt, i, :], ident_b[:]),
                                 reads=["S8b", "ident_b"], writes=[PK(pb)])
                        outap = sT[:, b, jt * 1024:(jt + 1) * 1024].rearrange("p (j i) -> p j i", i=8)
                        inap = psb.rearrange("p (i j) -> p j i", i=8)
                        P.op("act", lambda outap=outap, inap=inap: nc.scalar.copy(out=outap, in_=inap), reads=[PK(pb)], writes=[("sT", "all")])
                P.barrier()

            if l == 0:
                dump("sT0", sT[:], [128, 4, S], BF16)
            checkpoint(10 * l + 3)
            with ExitStack() as st:
                wg = sb("wglu" + L, [128, 4, 512], BF16, st)
                for c in range(2):
                    P.dma("pool", "wglu%d" % c, wg[:, :, c * 256:(c + 1) * 256],
                          p["w_glu"][l][:, c * 256:(c + 1) * 256].rearrange("(kt p) c -> p kt c", p=128), writes=[("wglu", c)])
                sig = [sb("gsig%d%s" % (i, L), [128, CH], BF16, st) for i in range(2)]
                for c in range(NCH):
                    cs = slice(c * CH, (c + 1) * CH)
                    for mt in range(4):
                        for kt in range(4):
                            P.op("pe", lambda mt=mt, kt=kt, cs=cs: nc.tensor.matmul(ps[mt][:, :], lhsT=wg[:, kt, mt * 128:(mt + 1) * 128],
                                                                                   rhs=sT[:, kt, cs], start=(kt == 0), stop=(kt == 3)),
                                 reads=[("wglu", 0), ("wglu", 1), ("sT", "all")], writes=[PK(mt)], signal=(kt == 3))
                    for mt in range(4):
                        sg_ = sig[mt % 2]
                        P.op("act", lambda mt=mt, sg_=sg_: nc.scalar.activation(out=sg_[:], in_=ps[mt][:, :], func=AF.Sigmoid),
                             reads=[PK(mt)], writes=[("gsig", mt % 2)])
                        P.op("dve", lambda mt=mt, sg_=sg_, cs=cs: nc.vector.tensor_tensor(out=sT[:, mt, cs], in0=sT[:, mt, cs], in1=sg_[:], op=ALU.mult),
                             reads=[("gsig", mt % 2), ("sT", "all")], writes=[("sT", "all")])
                P.barrier()

            checkpoint(10 * l + 4)
            with ExitStack() as stB:
                oT = sb("oT" + L, [128, 4, S], BF16, stB)
                with ExitStack() as st:
                    qT = sb("qT" + L, [128, 4, S], BF16, st)
                    kT = sb("kT" + L, [128, 4, S], BF16, st)
                    Vt = sb("Vt" + L, [128, 16, 512], BF16, st)
                    gains = sb("gains" + L, [128, 2], F32, st)
                    for half in range(2):
                        P.dma("sp", "gains", gains[half * 64:(half + 1) * 64, 0:1], p["q_gain"][l].rearrange("(p o) -> p o", o=1), writes=["gains"])
                        P.dma("sp", "gains", gains[half * 64:(half + 1) * 64, 1:2], p["k_gain"][l].rearrange("(p o) -> p o", o=1), writes=["gains"])
                    lamv = sb("lamv" + L, [128, 4, 64], F32, st)
                    for i_, nm_ in enumerate(("lambda_q1", "lambda_k1", "lambda_q2", "lambda_k2")):
                        P.dma("sp", "lamv", lamv[:, i_, :], p[nm_][l:l + 1, :].partition_broadcast(128), writes=["lamv"])
                    lam2 = sb("lam2" + L, [128, 2, 64], F32, st)
                    lsum = sb("lsum" + L, [128, 2], F32, st)
                    neglam = sb("neglam" + L, [128, 1], F32, st)
                    sublw = sb("sublw" + L, [128, 1], F32, st)
                    P.dma("sp", "sublw", sublw[:], p["subln"][l].rearrange("(p o) -> p o", o=1), writes=["sublw"])
                    P.op("dve", lambda: nc.vector.tensor_tensor(out=lam2[:, 0, :], in0=lamv[:, 0, :], in1=lamv[:, 1, :], op=ALU.mult),
                         reads=["lamv"], writes=["lam2"])
                    P.op("dve", lambda: nc.vector.tensor_tensor(out=lam2[:, 1, :], in0=lamv[:, 2, :], in1=lamv[:, 3, :], op=ALU.mult),
                         reads=["lamv"], writes=["lam2"])
                    P.op("dve", lambda: nc.vector.reduce_sum(out=lsum[:], in_=lam2[:], axis=AX.X), reads=["lam2"], writes=["lsum"])
                    P.op("act", lambda: nc.scalar.activation(out=lsum[:], in_=lsum[:], func=AF.Exp), reads=["lsum"], writes=["lsum"])
                    P.op("dve", lambda: nc.vector.tensor_tensor(out=neglam[:], in0=lsum[:, 1:2], in1=lsum[:, 0:1], op=ALU.subtract),
                         reads=["lsum"], writes=["neglam"])
                    P.op("dve", lambda: nc.vector.tensor_scalar(out=neglam[:], in0=neglam[:], scalar1=-lam_init, scalar2=None, op0=ALU.add),
                         reads=["neglam"], writes=["neglam"])
                    P.op("dve", lambda: nc.vector.tensor_scalar(out=sublw[:], in0=sublw[:], scalar1=1.0 - lam_init, scalar2=None, op0=ALU.mult),
                         reads=["sublw"], writes=["sublw"])

                    stp_ = ExitStack()
                    wsl = [sb("wsA%d%s" % (i, L), [128, 8, 256], BF16, stp_) for i in range(2)]
                    wkeys = ["wsA%d" % i for i in range(2)]
                    NQB = 3
                    raw = [sb("qraw%d%s" % (i, L), [128, CH], F32, stp_) for i in range(NQB)]
                    sqb = [sb("qsq%d%s" % (i, L), [128, CH], BF16, stp_) for i in range(NQB)]
                    rs = [sb("qrs%d%s" % (i, L), [128, CH], F32, stp_) for i in range(NQB)]
                    PA_B = [0, 1, 4]
                    PN_B = [2, 3, 5]
                    wst = {"i": 0}
                    it = 0
                    for which in range(2):
                        dstT = qT if which == 0 else kT
                        for cc in range(2):
                            wt, wk = load_w(wsl, wkeys, w_in_l[:, which * 512 + cc * 256: which * 512 + (cc + 1) * 256], 8, 256, wst)
                            for hh in range(2):
                                h = cc * 2 + hh
                                for c in range(NCH):
                                    cs = slice(c * CH, (c + 1) * CH)
                                    qi = it % NQB
                                    pa = PA_B[qi]
                                    pn = PN_B[qi]
                                    it += 1
                                    for kt in range(8):
                                        P.op("pe", lambda kt=kt, wt=wt, hh=hh, cs=cs, pa=pa: nc.tensor.matmul(
                                            ps[pa][:, :], lhsT=wt[:, kt, hh * 128:(hh + 1) * 128], rhs=hT[:, kt, cs], start=(kt == 0), stop=(kt == 7)),
                                            reads=[wk, ("hT", c)], writes=[PK(pa)], signal=(kt == 7))
                                    P.op("act", lambda pa=pa, qi=qi: nc.scalar.activation(out=sqb[qi][:], in_=ps[pa][:, :], func=AF.Square),
                                         reads=[PK(pa)], writes=[("qsq", qi)])
                                    P.op("act", lambda pa=pa, qi=qi, which=which: nc.scalar.activation(out=raw[qi][:], in_=ps[pa][:, :], func=AF.Copy,
                                                                                                       scale=gains[:, which:which + 1]),
                                         reads=[PK(pa), "gains"], writes=[("qraw", qi)])
                                    P.op("pe", lambda qi=qi, pn=pn: nc.tensor.matmul(ps[pn][:, :], lhsT=blk_b[:], rhs=sqb[qi][:], start=True, stop=True),
                                         reads=[("qsq", qi), "blk_b"], writes=[PK(pn)])
                                    P.op("act", lambda qi=qi, pn=pn: nc.scalar.activation(out=rs[qi][:], in_=ps[pn][:, :], func=AF.Sqrt, scale=1.0 / 64, bias=epsb[:, 0:1]),
                                         reads=[PK(pn), "epsb"], writes=[("qrs", qi)])
                                    P.op("dve", lambda qi=qi: nc.vector.reciprocal(out=rs[qi][:], in_=rs[qi][:]),
                                         reads=[("qrs", qi)], writes=[("qrs", qi)])
                                    P.op("pool", lambda qi=qi, h=h, cs=cs, dstT=dstT: nc.gpsimd.tensor_tensor(
                                        out=dstT[:, h, cs], in0=raw[qi][:], in1=rs[qi][:], op=ALU.mult),
                                        reads=[("qraw", qi), ("qrs", qi)], writes=[("qk", which, h, c)])
                    wv = []
                    for cc in range(2):
                        wv.append(load_w(wsl, wkeys, w_in_l[:, 1024 + cc * 256:1024 + (cc + 1) * 256], 8, 256, wst))
                    for tt in range(16):
                        pa = tt % 2
                        for cc in range(2):
                            wt, wk = wv[cc]
                            for kt in range(8):
                                P.op("pe", lambda kt=kt, wt=wt, cc=cc, tt=tt, pa=pa: nc.tensor.matmul(
                                    ps[pa][:, cc * 256:(cc + 1) * 256], lhsT=hT[:, kt, tt * 128:(tt + 1) * 128], rhs=wt[:, kt, :],
                                    start=(kt == 0), stop=(kt == 7)),
                                    reads=[wk, ("hT", tt // 4)], writes=[PK(pa)], signal=(kt == 7))
                        if tt % 2 == 0:
                            P.op("act", lambda tt=tt, pa=pa: nc.scalar.copy(out=Vt[:, tt, :], in_=ps[pa][:, :]), reads=[PK(pa)], writes=[("Vt", tt)])
                        else:
                            P.op("dve", lambda tt=tt, pa=pa: nc.vector.tensor_copy(out=Vt[:, tt, :], in_=ps[pa][:, :]), reads=[PK(pa)], writes=[("Vt", tt)])

                    P.barrier()
                    stp_.close()
                    checkpoint(10 * l + 5, [st, stB])
                    hank = sb("hank" + L, [128, NH, HW_], BF16, st)
                    for h in range(NH):
                        src = bass.AP(tensor=tb_scr, offset=h * (HW_ + 128), ap=[[1, 128], [1, HW_]])
                        P.dma("sp", "hank", hank[:, h, :], src, writes=["hank"])
                    pt = [sb("pt%d%s" % (i, L), [128, CH], BF16, st) for i in range(4)]
                    fa = sb("fa" + L, [128, CH], F32, st)
                    fb = sb("fb" + L, [128, CH], F32, st)
                    fo = sb("fo" + L, [128, CH], F32, st)
                    fr = sb("fr" + L, [128, CH], F32, st)
                    fsq = sb("fsq" + L, [128, CH], BF16, st)
                    SCB = [0, 1, 7] if os.environ.get("KSCB", "3") == "3" else [0, 1]
                    state = {"pti": 0, "sci": 0}
                    deferred = []

                    def emit_qk(h, Q, m, kt):
                        r = kt - 4 * Q
                        c0 = 128 * r if r > 0 else 0
                        N = CH - c0
                        off = CH * Q + c0 - 128 * kt
                        near = off < 256
                        sb_ = SCB[state["sci"] % len(SCB)]
                        state["sci"] += 1
                        pi_ = state["pti"] % 4
                        state["pti"] += 1
                        p_ = pt[pi_]
                        pk_ = ("pt", pi_)
                        ms = slice(m * 64, (m + 1) * 64)
                        P.op("pe", lambda: nc.tensor.matmul(
                            ps[sb_][:, 0:N], lhsT=kT[ms, h, kt * 128:(kt + 1) * 128], rhs=qT[ms, h, Q * CH + c0:(Q + 1) * CH],
                            start=True, stop=(not near)),
                            reads=[("qk", 1, h, kt // 4), ("qk", 0, h, Q)], writes=[PK(sb_)], signal=(not near))
                        if near:
                            P.op("pe", lambda: nc.tensor.matmul(
                                ps[sb_][:, 0:N], lhsT=antiid_b[:], rhs=hank[:, h, off:off + N], start=False, stop=True),
                                reads=["antiid_b", "hank"], writes=[PK(sb_)])
                            P.op("act", lambda: nc.scalar.activation(out=p_[:, 0:N], in_=ps[sb_][:, 0:N], func=AF.Exp, scale=0.125),
                                 reads=[PK(sb_)], writes=[pk_])
                        else:
                            P.op("act", lambda: nc.scalar.activation(out=p_[:, 0:N], in_=ps[sb_][:, 0:N], func=AF.Exp,
                                                                     scale=0.125, bias=cbias[:, h:h + 1]),
                                 reads=[PK(sb_), "cbias"], writes=[pk_])
                        return (p_, pk_, c0, N)

                    def emit_av(h, Q, m, kt, blk):
                        p_, pk_, c0, N = blk
                        Ob, Db = 2 + m, 4 + m
                        nkt = 4 * Q + 4
                        P.op("pe", lambda: nc.tensor.matmul(
                            ps[Ob][:, c0:CH], lhsT=Vt[:, kt, h * 128:(h + 1) * 128], rhs=p_[:, 0:N], start=(kt == 0), stop=(kt == nkt - 1)),
                            reads=[("Vt", kt), pk_], writes=[PK(Ob)], signal=False)
                        P.op("pe", lambda: nc.tensor.matmul(
                            ps[Db][:, c0:CH], lhsT=ones_b[:], rhs=p_[:, 0:N], start=(kt == 0), stop=(kt == nkt - 1)),
                            reads=["ones_b", pk_], writes=[PK(Db)], signal=True)

                    def epilogue_a(h, Q):
                        P.op("act", lambda: nc.scalar.activation(out=fr[:], in_=ps[4][:, :], func=AF.Ln), reads=[PK(4)], writes=["fr"])
                        P.op("act", lambda: nc.scalar.activation(out=fr[:], in_=fr[:], func=AF.Exp, scale=-1.0), reads=["fr"], writes=["fr"])
                        P.op("dve", lambda: nc.vector.tensor_tensor(out=fa[:], in0=ps[2][:, :], in1=fr[:], op=ALU.mult), reads=[PK(2), "fr"], writes=["fa"])
                        P.op("act", lambda: nc.scalar.activation(out=fb[:], in_=ps[5][:, :], func=AF.Ln), reads=[PK(5)], writes=["fb"])
                        P.op("act", lambda: nc.scalar.activation(out=fb[:], in_=fb[:], func=AF.Exp, scale=-1.0), reads=["fb"], writes=["fb"])
                        P.op("dve", lambda: nc.vector.tensor_tensor(out=fb[:], in0=ps[3][:, :], in1=fb[:], op=ALU.mult), reads=[PK(3), "fb"], writes=["fb"])
                        P.op("dve", lambda: nc.vector.scalar_tensor_tensor(out=fo[:], in0=fb[:], scalar=neglam[:, 0:1], in1=fa[:], op0=ALU.mult, op1=ALU.add),
                             reads=["fa", "fb", "neglam"], writes=["fo"])
                        P.op("act", lambda: nc.scalar.activation(out=fsq[:], in_=fo[:], func=AF.Square), reads=["fo"], writes=["fsq"])

                    def epilogue_b(h, Q):
                        cs = slice(Q * CH, (Q + 1) * CH)
                        P.op("pe", lambda: nc.tensor.matmul(ps[6][:, :], lhsT=ones_b[:], rhs=fsq[:], start=True, stop=True), reads=["fsq", "ones_b"], writes=[PK(6)])
                        P.op("act", lambda: nc.scalar.activation(out=fa[:], in_=ps[6][:, :], func=AF.Ln, scale=1.0 / 128, bias=epsb[:, 0:1]),
                             reads=[PK(6), "epsb"], writes=["fa"])
                        P.op("act", lambda: nc.scalar.activation(out=fa[:], in_=fa[:], func=AF.Exp, scale=-0.5), reads=["fa"], writes=["fa"])
                        P.op("dve", lambda: nc.vector.scalar_tensor_tensor(out=oT[:, h, cs], in0=fo[:], scalar=sublw[:, 0:1], in1=fa[:],
                                                                           op0=ALU.mult, op1=ALU.mult),
                             reads=["fo", "fa", "sublw"], writes=[("oT", "all")])

                    for h in range(NH):
                        for Q in range(NCH):
                            blocks = [(m, kt) for m in range(2) for kt in range(4 * Q + 4)]
                            pend = []
                            for bi, (m, kt) in enumerate(blocks):
                                cur = emit_qk(h, Q, m, kt)
                                pend.append((m, kt, cur))
                                if len(pend) > 2:
                                    pm, pkt, pblk = pend.pop(0)
                                    emit_av(h, Q, pm, pkt, pblk)
                                if bi == 3 and deferred:
                                    deferred.pop(0)()
                            while pend:
                                pm, pkt, pblk = pend.pop(0)
                                emit_av(h, Q, pm, pkt, pblk)
                            while deferred:
                                deferred.pop(0)()
                            epilogue_a(h, Q)
                            deferred.append(lambda h=h, Q=Q: epilogue_b(h, Q))
                    while deferred:
                        deferred.pop(0)()
                    P.barrier()

                checkpoint(10 * l + 6, [stB])
                with ExitStack() as st:
                    mixT = sb("mixT" + L, [128, 8, S], BF16, st)
                    wsl = [sb("wsM%d%s" % (i, L), [128, 8, 256], BF16, st) for i in range(4)]
                    wkeys = ["wsM%d" % i for i in range(4)]
                    wsl2 = [sb("wsN%d%s" % (i, L), [128, 4, 256], BF16, st) for i in range(4)]
                    wkeys2 = ["wsN%d" % i for i in range(4)]
                    sg = [sb("msg%d%s" % (i, L), [128, CH], F32, st) for i in range(2)]
                    m1 = sb("m1" + L, [128, CH], F32, st)
                    m2 = sb("m2" + L, [128, CH], F32, st)
                    wst = {"i": 0}
                    wst2 = {"i": 0}
                    for cc in range(4):
                        wga, kga = load_w(wsl, wkeys, w_in_l[:, 2048 + cc * 256:2048 + (cc + 1) * 256], 8, 256, wst)
                        wgb, kgb = load_w(wsl, wkeys, w_in_l[:, 3072 + cc * 256:3072 + (cc + 1) * 256], 8, 256, wst)
                        wa, ka = load_w(wsl2, wkeys2, p["w_a"][l][:, cc * 256:(cc + 1) * 256], 4, 256, wst2)
                        wb, kb = load_w(wsl2, wkeys2, p["w_b"][l][:, cc * 256:(cc + 1) * 256], 4, 256, wst2)
                        for hh in range(2):
                            mt = cc * 2 + hh
                            ms = slice(hh * 128, (hh + 1) * 128)
                            for c in range(NCH):
                                cs = slice(c * CH, (c + 1) * CH)
                                pb = 4 * (c % 2)
                                for kt in range(8):
                                    P.op("pe", lambda kt=kt, cs=cs, pb=pb, ms=ms, wga=wga: nc.tensor.matmul(ps[pb][:, :], lhsT=wga[:, kt, ms], rhs=hT[:, kt, cs],
                                                                                                          start=(kt == 0), stop=(kt == 7)),
                                         reads=[kga, ("hT", c)], writes=[PK(pb)], signal=(kt == 7))
                                for kt in range(8):
                                    P.op("pe", lambda kt=kt, cs=cs, pb=pb, ms=ms, wgb=wgb: nc.tensor.matmul(ps[pb + 1][:, :], lhsT=wgb[:, kt, ms], rhs=hT[:, kt, cs],
                                                                                                          start=(kt == 0), stop=(kt == 7)),
                                         reads=[kgb, ("hT", c)], writes=[PK(pb + 1)], signal=(kt == 7))
                                for kt in range(4):
                                    P.op("pe", lambda kt=kt, cs=cs, pb=pb, ms=ms, wa=wa: nc.tensor.matmul(ps[pb + 2][:, :], lhsT=wa[:, kt, ms], rhs=oT[:, kt, cs],
                                                                                                        start=(kt == 0), stop=(kt == 3)),
                                         reads=[ka, ("oT", "all")], writes=[PK(pb + 2)], signal=(kt == 3))
                                for kt in range(4):
                                    P.op("pe", lambda kt=kt, cs=cs, pb=pb, ms=ms, wb=wb: nc.tensor.matmul(ps[pb + 3][:, :], lhsT=wb[:, kt, ms], rhs=sT[:, kt, cs],
                                                                                                        start=(kt == 0), stop=(kt == 3)),
                                         reads=[kb, ("sT", "all")], writes=[PK(pb + 3)], signal=(kt == 3))
                                P.op("act", lambda pb=pb: nc.scalar.activation(out=sg[0][:], in_=ps[pb][:, :], func=AF.Sigmoid), reads=[PK(pb)], writes=[("msg", 0)])
                                P.op("act", lambda pb=pb: nc.scalar.activation(out=sg[1][:], in_=ps[pb + 1][:, :], func=AF.Sigmoid), reads=[PK(pb + 1)], writes=[("msg", 1)])
                                P.op("dve", lambda pb=pb: nc.vector.tensor_tensor(out=m1[:], in0=ps[pb + 2][:, :], in1=sg[0][:], op=ALU.mult),
                                     reads=[PK(pb + 2), ("msg", 0)], writes=["m1"])
                                P.op("dve", lambda pb=pb: nc.vector.tensor_tensor(out=m2[:], in0=ps[pb + 3][:, :], in1=sg[1][:], op=ALU.mult),
                                     reads=[PK(pb + 3), ("msg", 1)], writes=["m2"])
                                P.op("dve", lambda mt=mt, cs=cs: nc.vector.tensor_tensor(out=mixT[:, mt, cs], in0=m1[:], in1=m2[:], op=ALU.add),
                                     reads=["m1", "m2"], writes=[("mixT", c)])
                    for cc in range(4):
                        wo, ko = load_w(wsl, wkeys, p["w_o"][l][:, cc * 256:(cc + 1) * 256], 8, 256, wst)
                        for hh in range(2):
                            mt = cc * 2 + hh
                            ms = slice(hh * 128, (hh + 1) * 128)
                            for c in range(NCH):
                                cs = slice(c * CH, (c + 1) * CH)
                                pb = (mt * NCH + c) % 4
                                for kt in range(8):
                                    P.op("pe", lambda kt=kt, cs=cs, pb=pb, ms=ms, wo=wo: nc.tensor.matmul(ps[pb][:, :], lhsT=wo[:, kt, ms], rhs=mixT[:, kt, cs],
                                                                                                        start=(kt == 0), stop=(kt == 7)),
                                         reads=[ko, ("mixT", c)], writes=[PK(pb)], signal=(kt == 7))
                                P.op("dve", lambda mt=mt, cs=cs, pb=pb: nc.vector.tensor_tensor(out=xT[:, mt, cs], in0=xT[:, mt, cs], in1=ps[pb][:, :], op=ALU.add),
                                     reads=[PK(pb), ("xT", "all")], writes=[("xT", "all")])
                    P.barrier()

            checkpoint(10 * l + 7)
            rmsnorm_to_hT(p["norm_ffn"][l], es, "f" + L)
            with ExitStack() as st:
                act = sb("act" + L, [128, 12, S], BF16, st)
                wsl = [sb("wsF%d%s" % (i, L), [128, 12, 256], BF16, st) for i in range(4)]
                wkeys = ["wsF%d" % i for i in range(4)]
                sl = [sb("fsl%d%s" % (i, L), [128, CH], F32, st) for i in range(2)]
                wst = {"i": 0}
                it = 0
                for hf in range(2):
                    mt0 = 0 if hf == 0 else 12
                    nmt = 12 if hf == 0 else 10
                    for cc in range(nmt // 2):
                        col = (mt0 + cc * 2) * 128
                        w1t, k1 = load_w(wsl, wkeys, p["w1"][l][:, col:col + 256], 8, 256, wst)
                        w3t, k3 = load_w(wsl, wkeys, p["w3"][l][:, col:col + 256], 8, 256, wst)
                        for hh in range(2):
                            ml = cc * 2 + hh
                            ms = slice(hh * 128, (hh + 1) * 128)
                            for c in range(NCH):
                                cs = slice(c * CH, (c + 1) * CH)
                                pb = 2 * (it % 2)
                                sl_ = sl[it % 2]
                                slk = ("fsl", it % 2)
                                it += 1
                                for kt in range(8):
                                    P.op("pe", lambda kt=kt, cs=cs, pb=pb, ms=ms, w1t=w1t: nc.tensor.matmul(ps[pb][:, :], lhsT=w1t[:, kt, ms], rhs=hT[:, kt, cs],
                                                                                                          start=(kt == 0), stop=(kt == 7)),
                                         reads=[k1, ("hT", c)], writes=[PK(pb)], signal=(kt == 7))
                                for kt in range(8):
                                    P.op("pe", lambda kt=kt, cs=cs, pb=pb, ms=ms, w3t=w3t: nc.tensor.matmul(ps[pb + 1][:, :], lhsT=w3t[:, kt, ms], rhs=hT[:, kt, cs],
                                                                                                          start=(kt == 0), stop=(kt == 7)),
                                         reads=[k3, ("hT", c)], writes=[PK(pb + 1)], signal=(kt == 7))
                                P.op("act", lambda pb=pb, sl_=sl_: nc.scalar.activation(out=sl_[:], in_=ps[pb][:, :], func=AF.Silu), reads=[PK(pb)], writes=[slk])
                                P.op("dve", lambda pb=pb, sl_=sl_, ml=ml, cs=cs: nc.vector.tensor_tensor(out=act[:, ml, cs], in0=sl_[:], in1=ps[pb + 1][:, :], op=ALU.mult),
                                     reads=[slk, PK(pb + 1)], writes=[("act", c)])
                    for cc in range(4):
                        w2t, k2 = load_w(wsl, wkeys, p["w2"][l][mt0 * 128:(mt0 + nmt) * 128, cc * 256:(cc + 1) * 256], nmt, 256, wst)
                        for hh in range(2):
                            mo = cc * 2 + hh
                            ms = slice(hh * 128, (hh + 1) * 128)
                            for c in range(NCH):
                                cs = slice(c * CH, (c + 1) * CH)
                                pb = 4 + (mo * NCH + c) % 4
                                for kt in range(nmt):
                                    P.op("pe", lambda kt=kt, cs=cs, pb=pb, ms=ms, w2t=w2t, nmt=nmt: nc.tensor.matmul(ps[pb][:, :], lhsT=w2t[:, kt, ms], rhs=act[:, kt, cs],
                                                                                                                   start=(kt == 0), stop=(kt == nmt - 1)),
                                         reads=[k2, ("act", c)], writes=[PK(pb)], signal=(kt == nmt - 1))
                                P.op("dve", lambda mo=mo, cs=cs, pb=pb: nc.vector.tensor_tensor(out=xT[:, mo, cs], in0=xT[:, mo, cs], in1=ps[pb][:, :], op=ALU.add),
                                     reads=[PK(pb), ("xT", "all")], writes=[("xT", "all")])
                P.barrier()

        for l in range(DEPTH):
            try:
                _layer(l)
            except _Stop:
                break
            except BaseException:
                import traceback
                traceback.print_exc()
                raise

        with ExitStack() as st:
            xo = [sb("xo%d" % i, [128, D], F32, st) for i in range(2)]
            for tt in range(16):
                b_ = xo[tt % 2]
                for half in range(2):
                    pb = (tt * 2 + half) % 4
                    for q in range(4):
                        kt = half * 4 + q
                        P.op("pe", lambda kt=kt, q=q, pb=pb, tt=tt: nc.tensor.transpose(
                            ps[pb][:, q * 128:(q + 1) * 128], xT[:, kt, tt * 128:(tt + 1) * 128], ident_f[:]),
                            reads=[("xT", "all"), "ident_f"], writes=[PK(pb)])
                    if half == 0:
                        P.op("dve", lambda b_=b_, pb=pb: nc.vector.tensor_copy(out=b_[:, 0:512], in_=ps[pb][:, :]), reads=[PK(pb)], writes=[("xo", tt % 2, 0)])
                    else:
                        P.op("act", lambda b_=b_, pb=pb: nc.scalar.copy(out=b_[:, 512:1024], in_=ps[pb][:, :]), reads=[PK(pb)], writes=[("xo", tt % 2, 1)])
                P.dma("sp", "xo%d" % (tt % 2), y_d[tt * 128:(tt + 1) * 128, :], b_[:], reads=[("xo", tt % 2, 0), ("xo", tt % 2, 1)], writes=[("y", tt)])
            P.finish()
    return nc, consts


_CACHE = {}


def kernel(**inputs):
    if "nc" not in _CACHE:
        _CACHE["nc"] = build_program()
    nc, consts = _CACHE["nc"]
    x = np.ascontiguousarray(np.asarray(inputs["x"], dtype=np.float32))
    B = x.shape[0]
    shared = {}
    for k, v in inputs.items():
        if k == "x":
            continue
        shared[k] = np.ascontiguousarray(np.asarray(v, dtype=np.float32))
    for k, v in consts.items():
        shared["c_" + k] = v
    in_maps = []
    for b in range(B):
        m = dict(shared)
        m["x"] = x[b]
        in_maps.append(m)
    res = run_bass_kernel_spmd(nc, in_maps, core_ids=list(range(B)))
    out = np.stack([np.asarray(res.results[b]["y"], dtype=np.float32) for b in range(B)], axis=0)
    return out
```

```python
import math
import os
from contextlib import ExitStack

import numpy as np
import ml_dtypes

import concourse.bass as bass
import concourse.mybir as mybir
from concourse.bass_utils import run_bass_kernel_spmd

F32 = mybir.dt.float32
BF16 = mybir.dt.bfloat16
AF = mybir.ActivationFunctionType
ALU = mybir.AluOpType
AX = mybir.AxisListType

D = 1024
S = 2048
DEPTH = 2
NH = 4
DFF = 2816
INW = 4096
G = 32
EPS = 1e-6
TWO_PI = 2.0 * math.pi
NCH = 4
CH = 512
HW_ = 768
NEG_BIG = -60000.0


class _Stop(Exception):
    pass


KSTOP = float(os.environ.get("KSTOP", "999"))


class Prog:
    def __init__(self, nc, es):
        self.nc = nc
        self.es = es
        self.eng = {"pe": nc.tensor, "act": nc.scalar, "dve": nc.vector, "pool": nc.gpsimd, "sp": nc.sync}
        self.sem = {}
        self.cnt = {}
        for e in self.eng:
            self.sem[e] = es.enter_context(nc.semaphore("c_" + e))
            self.cnt[e] = 0
        self.dsem = {}
        self.dcnt = {}
        self.waited = {e: {} for e in self.eng}
        self.last_w = {}
        self.readers = {}
        self.pending = {}

    def _wait(self, e, dep):
        kind, name, val = dep
        if kind == "eng" and name == e and e == "pe":
            return
        key = (kind, name)
        if self.waited[e].get(key, 0) >= val:
            return
        self.waited[e][key] = val
        sem = self.sem[name] if kind == "eng" else self.dsem[name]
        self.eng[e].wait_ge(sem, val)

    def _check_pending(self, e, reads, writes):
        for e2, lst in self.pending.items():
            if e2 == e:
                continue
            for (r_, w_) in lst:
                if (set(w_) & (set(reads) | set(writes))) or (set(r_) & set(writes)):
                    raise RuntimeError("pending (non-signalled) access on %s conflicts with op on %s: %s %s" % (e2, e, r_, w_))

    def _collect(self, reads, writes):
        deps = []
        for k in reads:
            if k in self.last_w:
                deps.append(self.last_w[k])
        for k in writes:
            if k in self.last_w:
                deps.append(self.last_w[k])
            deps.extend(self.readers.get(k, []))
        return deps

    def _commit(self, dep, reads, writes):
        for k in reads:
            self.readers.setdefault(k, []).append(dep)
        for k in writes:
            self.last_w[k] = dep
            self.readers[k] = []

    def op(self, e, fn, reads=(), writes=(), signal=True):
        self._check_pending(e, reads, writes)
        for d in self._collect(reads, writes):
            self._wait(e, d)
        ins = fn()
        if not signal:
            self.pending.setdefault(e, []).append((tuple(reads), tuple(writes)))
            return ins
        self.cnt[e] += 1
        ins.then_inc(self.sem[e], 1)
        dep = ("eng", e, self.cnt[e])
        for (r_, w_) in self.pending.get(e, []):
            self._commit(dep, r_, w_)
        self.pending[e] = []
        self._commit(dep, reads, writes)
        return ins

    def dma(self, q, slot, out, in_, reads=(), writes=()):
        if slot not in self.dsem:
            self.dsem[slot] = self.es.enter_context(self.nc.semaphore("d_" + slot))
            self.dcnt[slot] = 0
        self._check_pending(q, reads, writes)
        for d in self._collect(reads, writes):
            self._wait(q, d)
        ins = self.eng[q].dma_start(out=out, in_=in_)
        self.dcnt[slot] += 16
        ins.then_inc(self.dsem[slot], 16)
        self._commit(("dma", slot, self.dcnt[slot]), reads, writes)

    def barrier(self):
        for e in self.eng:
            for e2 in self.eng:
                if e2 != e and self.cnt[e2] > 0:
                    self._wait(e, ("eng", e2, self.cnt[e2]))
            for slot, v in self.dcnt.items():
                self._wait(e, ("dma", slot, v))
        self.last_w = {}
        self.readers = {}

    def finish(self):
        for e in ("sp", "act", "dve", "pool", "pe"):
            for e2 in self.eng:
                if e2 != e and self.cnt[e2] > 0:
                    self._wait(e, ("eng", e2, self.cnt[e2]))
            for slot, v in self.dcnt.items():
                self._wait(e, ("dma", slot, v))


def t5_bucket_np(n):
    n = np.maximum(n, 0)
    nf = np.maximum(n, 1).astype(np.float32)
    large = 16 + (np.log(nf / np.float32(16)) / np.float32(math.log(8.0)) * np.float32(16)).astype(np.int32)
    large = np.minimum(large, 31)
    return np.where(n < 16, n, large)


def host_consts():
    c = {}
    c["ident_f"] = np.eye(128, dtype=np.float32)
    c["ident_b"] = np.eye(128, dtype=np.float32).astype(ml_dtypes.bfloat16)
    c["antiid_b"] = np.eye(128, dtype=np.float32)[::-1].copy().astype(ml_dtypes.bfloat16)
    c["ones_b"] = np.ones((128, 128), np.float32).astype(ml_dtypes.bfloat16)
    blk = np.zeros((128, 128), np.float32)
    blk[:64, :64] = 1
    blk[64:, 64:] = 1
    c["blk_b"] = blk.astype(ml_dtypes.bfloat16)
    m = (np.arange(8)[:, None] > np.arange(8)[None, :]).astype(np.float32)
    c["negmask"] = -np.repeat(np.repeat(m, 16, 0), 16, 1)
    sg = np.concatenate([-np.ones(64), np.ones(64)]).astype(np.float32)
    c["sgn"] = np.stack([sg, -sg], 1)
    c["jidx"] = np.tile(np.arange(256, dtype=np.float32)[None], (128, 1))
    jm = np.ones((128, 256), np.float32)
    jm[:, 0] = 0
    c["jmask"] = jm
    n = np.arange(HW_ + 128) - 127
    oh = np.zeros((32, HW_ + 128), np.float32)
    bk = t5_bucket_np(n)
    for i, d in enumerate(n):
        if d >= 0:
            oh[bk[i], i] = 8.0
    c["bk_onehot"] = oh
    nm = np.zeros((4, HW_ + 128), np.float32)
    nm[:, n < 0] = NEG_BIG
    c["bk_negmask"] = nm
    return c


CONST_SHAPES = None


def build_program(dbg=False):
    nc = bass.Bass("TRN2", target_bir_lowering=False)
    consts = host_consts()

    def din(name, shape, dt=F32):
        return nc.dram_tensor(name, list(shape), dt, kind="ExternalInput").ap()

    x_d = din("x", [S, D])
    rel_bias_d = din("rel_bias", [32, NH])
    p = {}
    pshapes = {
        "norm_mix": [DEPTH, D], "w_in": [DEPTH, D, INW], "q_gain": [DEPTH, 64], "k_gain": [DEPTH, 64],
        "lambda_q1": [DEPTH, 64], "lambda_k1": [DEPTH, 64], "lambda_q2": [DEPTH, 64], "lambda_k2": [DEPTH, 64],
        "subln": [DEPTH, 128], "w_a": [DEPTH, 512, D], "lam_re": [DEPTH, G, 64], "lam_im": [DEPTH, G, 64],
        "b_re": [DEPTH, G, 64, 16], "b_im": [DEPTH, G, 64, 16], "c_re": [DEPTH, G, 16, 64], "c_im": [DEPTH, G, 16, 64],
        "d_skip": [DEPTH, 512], "log_step": [DEPTH, G], "w_glu": [DEPTH, 512, 512], "w_b": [DEPTH, 512, D],
        "w_o": [DEPTH, D, D], "norm_ffn": [DEPTH, D], "w1": [DEPTH, D, DFF], "w3": [DEPTH, D, DFF], "w2": [DEPTH, DFF, D],
    }
    for k, shp in pshapes.items():
        p[k] = din(k, shp)
    cd = {}
    for k, v in consts.items():
        cd[k] = din("c_" + k, v.shape, BF16 if v.dtype == ml_dtypes.bfloat16 else F32)
    y_d = nc.dram_tensor("y", [S, D], F32, kind="ExternalOutput").ap()
    tb_scr = nc.dram_tensor("tb_scr", [NH, HW_ + 128], BF16, kind="Internal")
    dbg_outs = {}

    es = ExitStack()
    with es:
        es.enter_context(nc.allow_non_contiguous_dma(reason="small param loads"))
        P = Prog(nc, es)

        def sb(name, shape, dt, st=es):
            return st.enter_context(nc.sbuf_tensor(name, list(shape), dt))

        xT = sb("xT", [128, 8, S], F32)
        hT = sb("hT", [128, 8, S], BF16)
        sT = sb("sT", [128, 4, S], BF16)
        ident_f = sb("ident_f", [128, 128], F32)
        ident_b = sb("ident_b", [128, 128], BF16)
        antiid_b = sb("antiid_b", [128, 128], BF16)
        ones_b = sb("ones_b", [128, 128], BF16)
        blk_b = sb("blk_b", [128, 128], BF16)
        negmask = sb("negmask", [128, 128], F32)
        sgn = sb("sgn", [128, 2], F32)
        jidx = sb("jidx", [128, 256], F32)
        jmask = sb("jmask", [128, 256], F32)
        cbias = sb("cbias", [128, NH], F32)
        epsb = sb("epsb", [128, 1], F32)
        ps = [es.enter_context(nc.psum_tensor("ps%d" % i, [128, 512], F32)) for i in range(8)]

        def PK(i):
            return ("ps", i)

        for nm_, t_ in (("ident_f", ident_f), ("ident_b", ident_b), ("antiid_b", antiid_b), ("ones_b", ones_b),
                        ("blk_b", blk_b), ("negmask", negmask), ("sgn", sgn), ("jidx", jidx), ("jmask", jmask)):
            P.dma("sp", "c_" + nm_, t_[:], cd[nm_], writes=[nm_])
        P.op("dve", lambda: nc.vector.memset(epsb[:], EPS), writes=["epsb"])
        CK = ["ident_f", "ident_b", "antiid_b", "ones_b", "blk_b", "negmask", "sgn", "jidx", "jmask"]

        with ExitStack() as st:
            rb = sb("rb", [32, NH], F32, st)
            oh = sb("oh", [32, HW_ + 128], F32, st)
            nmk = sb("nmk", [4, HW_ + 128], F32, st)
            tbs = sb("tbs", [4, HW_ + 128], BF16, st)
            P.dma("sp", "rb", rb[:], rel_bias_d, writes=["rb"])
            P.dma("sp", "oh", oh[:], cd["bk_onehot"], writes=["oh"])
            P.dma("sp", "nmk", nmk[:], cd["bk_negmask"], writes=["nmk"])
            W_ = HW_ + 128
            for c0 in range(0, W_, 448):
                n_ = min(448, W_ - c0)
                P.op("pe", lambda c0=c0, n_=n_: nc.tensor.matmul(ps[0][0:4, 0:n_], lhsT=rb[:, :], rhs=oh[:, c0:c0 + n_],
                                                                 start=True, stop=True),
                     reads=["rb", "oh"], writes=[PK(0)])
                P.op("dve", lambda c0=c0, n_=n_: nc.vector.tensor_tensor(out=tbs[:, c0:c0 + n_], in0=ps[0][0:4, 0:n_],
                                                                          in1=nmk[:, c0:c0 + n_], op=ALU.add),
                     reads=[PK(0), "nmk"], writes=["tbs"])
            P.dma("sp", "tbw", tb_scr.ap(), tbs[:], reads=["tbs"], writes=["tb_scr"])
            P.dma("sp", "cbias", cbias[:], rel_bias_d[31:32, :].partition_broadcast(128), writes=["cbias"])
            P.barrier()

        with ExitStack() as st:
            xin = [sb("xin%d" % i, [128, D], F32, st) for i in range(2)]
            for tt in range(16):
                b_ = xin[tt % 2]
                P.dma("sp", "xin%d" % (tt % 2), b_[:], x_d[tt * 128:(tt + 1) * 128, :], writes=[("xin", tt % 2)])
                for half in range(2):
                    pb = (tt * 2 + half) % 4
                    for q in range(4):
                        kt = half * 4 + q
                        P.op("pe", lambda kt=kt, q=q, pb=pb, b_=b_: nc.tensor.transpose(
                            ps[pb][:, q * 128:(q + 1) * 128], b_[:, kt * 128:(kt + 1) * 128], ident_f[:]),
                            reads=[("xin", tt % 2), "ident_f"], writes=[PK(pb)])
                    eng = "dve" if half == 0 else "act"
                    outap = xT[:, half * 4:(half + 1) * 4, tt * 128:(tt + 1) * 128]
                    inap = ps[pb][:, :].rearrange("p (q t) -> p q t", q=4)
                    if eng == "dve":
                        P.op("dve", lambda outap=outap, inap=inap: nc.vector.tensor_copy(out=outap, in_=inap),
                             reads=[PK(pb)], writes=[("xT", "all")])
                    else:
                        P.op("act", lambda outap=outap, inap=inap: nc.scalar.copy(out=outap, in_=inap),
                             reads=[PK(pb)], writes=[("xT", "all")])
            P.barrier()

        wq = {"i": 0}

        def rmsnorm_to_hT(gvec_d, st_parent, tag):
            with ExitStack() as st:
                gT = sb("gT" + tag, [128, 8], F32, st)
                P.dma("sp", "gT", gT[:], gvec_d.rearrange("(kt p) -> p kt", p=128), writes=["gT"])
                sq = [sb("nsq%d%s" % (i, tag), [128, CH], BF16, st) for i in range(2)]
                rstds = [sb("nrstd%d%s" % (i, tag), [128, CH], F32, st) for i in range(2)]
                for c in range(NCH):
                    cs = slice(c * CH, (c + 1) * CH)
                    rstd = rstds[c % 2]
                    krs = ("nrstd", c % 2)
                    for kt in range(8):
                        b_ = sq[kt % 2]
                        P.op("act", lambda kt=kt, b_=b_, cs=cs: nc.scalar.activation(out=b_[:], in_=xT[:, kt, cs], func=AF.Square),
                             reads=[("xT", "all")], writes=[("nsq", kt % 2)])
                        P.op("pe", lambda kt=kt, b_=b_, c=c: nc.tensor.matmul(ps[c % 2][:, :], lhsT=ones_b[:], rhs=b_[:],
                                                                         start=(kt == 0), stop=(kt == 7)),
                             reads=[("nsq", kt % 2), "ones_b"], writes=[PK(c % 2)])
                    P.op("act", lambda c=c, rstd=rstd: nc.scalar.activation(out=rstd[:], in_=ps[c % 2][:, :], func=AF.Ln, scale=1.0 / D, bias=epsb[:, 0:1]),
                         reads=[PK(c % 2), "epsb"], writes=[krs])
                    P.op("act", lambda: nc.scalar.activation(out=rstd[:], in_=rstd[:], func=AF.Exp, scale=-0.5),
                         reads=[krs], writes=[krs])
                    for kt in range(8):
                        P.op("dve", lambda kt=kt, cs=cs: nc.vector.scalar_tensor_tensor(
                            out=hT[:, kt, cs], in0=xT[:, kt, cs], scalar=gT[:, kt:kt + 1], in1=rstd[:],
                            op0=ALU.mult, op1=ALU.mult),
                            reads=[("xT", "all"), "gT", krs], writes=[("hT", c)])
                P.barrier()

        def load_w(slots, slot_keys, src_ap, nk, ncols, wstate):
            i = wstate["i"] % len(slots)
            wstate["i"] += 1
            dst = slots[i][:, 0:nk, 0:ncols]
            P.dma("pool", slot_keys[i], dst, src_ap.rearrange("(kt p) c -> p kt c", p=128), writes=[("w", slot_keys[i])])
            return slots[i], ("w", slot_keys[i])

        def expsmall(out_ap, x_ap, tmp_ap, keys_r, key_w, key_t):
            coef = [1.0 / math.factorial(k) for k in range(10)]
            P.op("dve", lambda: nc.vector.tensor_scalar(out=tmp_ap, in0=x_ap, scalar1=coef[9], scalar2=coef[8],
                                                        op0=ALU.mult, op1=ALU.add),
                 reads=keys_r, writes=[key_t])
            for k in range(7, -1, -1):
                P.op("dve", lambda: nc.vector.tensor_tensor(out=tmp_ap, in0=tmp_ap, in1=x_ap, op=ALU.mult),
                     reads=keys_r + [key_t], writes=[key_t])
                dst = out_ap if k == 0 else tmp_ap
                P.op("dve", lambda k=k, dst=dst: nc.vector.tensor_scalar(out=dst, in0=tmp_ap, scalar1=coef[k], scalar2=None,
                                                                          op0=ALU.add),
                     reads=[key_t], writes=[key_w if k == 0 else key_t])

        C1 = 6.28125
        C2 = TWO_PI - 6.28125

        def sin_pos(out_ap, x_ap, tmp_ap, tmp2_ap, mult, add, keys_r, key_w, key_t, key_t2, do_sin=True):
            ki = tmp2_ap.bitcast(mybir.dt.int32)
            P.op("dve", lambda: nc.vector.tensor_scalar(out=ki, in0=x_ap, scalar1=float(mult) / TWO_PI, scalar2=float(add) / TWO_PI,
                                                        op0=ALU.mult, op1=ALU.add),
                 reads=keys_r, writes=[key_t2])
            P.op("dve", lambda: nc.vector.tensor_copy(out=tmp_ap, in_=ki), reads=[key_t2], writes=[key_t])
            P.op("dve", lambda: nc.vector.tensor_scalar(out=tmp2_ap, in0=x_ap, scalar1=float(mult), scalar2=float(add),
                                                        op0=ALU.mult, op1=ALU.add),
                 reads=keys_r + [key_t], writes=[key_t2])
            P.op("dve", lambda: nc.vector.scalar_tensor_tensor(out=tmp2_ap, in0=tmp_ap, scalar=-C1, in1=tmp2_ap, op0=ALU.mult, op1=ALU.add),
                 reads=[key_t, key_t2], writes=[key_t2])
            P.op("dve", lambda: nc.vector.scalar_tensor_tensor(out=tmp2_ap, in0=tmp_ap, scalar=-C2, in1=tmp2_ap, op0=ALU.mult, op1=ALU.add),
                 reads=[key_t, key_t2], writes=[key_t2])
            P.op("dve", lambda: nc.vector.tensor_scalar(out=tmp_ap, in0=tmp2_ap, scalar1=math.pi, scalar2=-TWO_PI, op0=ALU.is_gt, op1=ALU.mult),
                 reads=[key_t2], writes=[key_t])
            P.op("dve", lambda: nc.vector.tensor_tensor(out=tmp2_ap, in0=tmp2_ap, in1=tmp_ap, op=ALU.add), reads=[key_t, key_t2], writes=[key_t2])
            P.op("dve", lambda: nc.vector.tensor_scalar(out=tmp_ap, in0=tmp2_ap, scalar1=-math.pi, scalar2=TWO_PI, op0=ALU.is_lt, op1=ALU.mult),
                 reads=[key_t2], writes=[key_t])
            P.op("dve", lambda: nc.vector.tensor_tensor(out=tmp2_ap, in0=tmp2_ap, in1=tmp_ap, op=ALU.add), reads=[key_t, key_t2], writes=[key_t2])
            P.op("dve", lambda: nc.vector.tensor_scalar(out=tmp2_ap, in0=tmp2_ap, scalar1=-math.pi, scalar2=math.pi, op0=ALU.max, op1=ALU.min),
                 reads=[key_t2], writes=[key_t2])
            if do_sin:
                P.op("act", lambda: nc.scalar.activation(out=out_ap, in_=tmp2_ap, func=AF.Sin), reads=[key_t2], writes=[key_w])
            else:
                P.op("dve", lambda: nc.vector.tensor_copy(out=out_ap, in_=tmp2_ap), reads=[key_t2], writes=[key_w])

        halfpi = sb("halfpi", [128, 1], F32)
        P.op("dve", lambda: nc.vector.memset(halfpi[:], math.pi / 2), writes=["halfpi"])

        def sincos_pos(sin_ap, cos_ap, x_ap, tA, tB, mult, add, keys_r, key_s, key_c, key_a, key_b):
            ki = tA.bitcast(mybir.dt.int32)
            P.op("dve", lambda: nc.vector.tensor_scalar(out=ki, in0=x_ap, scalar1=float(mult) / TWO_PI, scalar2=float(add) / TWO_PI,
                                                        op0=ALU.mult, op1=ALU.add), reads=keys_r, writes=[key_a])
            P.op("dve", lambda: nc.vector.tensor_copy(out=tB, in_=ki), reads=[key_a], writes=[key_b])
            if mult == 1.0 and add == 0.0:
                P.op("dve", lambda: nc.vector.scalar_tensor_tensor(out=tA, in0=tB, scalar=-C1, in1=x_ap, op0=ALU.mult, op1=ALU.add),
                     reads=keys_r + [key_b], writes=[key_a])
            else:
                P.op("dve", lambda: nc.vector.tensor_scalar(out=tA, in0=x_ap, scalar1=float(mult), scalar2=float(add),
                                                            op0=ALU.mult, op1=ALU.add), reads=keys_r + [key_b], writes=[key_a])
                P.op("dve", lambda: nc.vector.scalar_tensor_tensor(out=tA, in0=tB, scalar=-C1, in1=tA, op0=ALU.mult, op1=ALU.add),
                     reads=[key_a, key_b], writes=[key_a])
            P.op("dve", lambda: nc.vector.scalar_tensor_tensor(out=tA, in0=tB, scalar=-C2, in1=tA, op0=ALU.mult, op1=ALU.add),
                 reads=[key_a, key_b], writes=[key_a])
            P.op("dve", lambda: nc.vector.tensor_scalar(out=tB, in0=tA, scalar1=math.pi, scalar2=-TWO_PI, op0=ALU.is_gt, op1=ALU.mult),
                 reads=[key_a], writes=[key_b])
            P.op("dve", lambda: nc.vector.tensor_tensor(out=tA, in0=tA, in1=tB, op=ALU.add), reads=[key_a, key_b], writes=[key_a])
            P.op("dve", lambda: nc.vector.tensor_scalar(out=tA, in0=tA, scalar1=-math.pi, scalar2=math.pi, op0=ALU.max, op1=ALU.min),
                 reads=[key_a], writes=[key_a])
            P.op("act", lambda: nc.scalar.activation(out=sin_ap, in_=tA, func=AF.Sin), reads=[key_a], writes=[key_s])
            P.op("dve", lambda: nc.vector.tensor_scalar(out=tB, in0=tA, scalar1=-1.0, scalar2=None, op0=ALU.mult), reads=[key_a], writes=[key_b])
            P.op("dve", lambda: nc.vector.tensor_tensor(out=tB, in0=tB, in1=tA, op=ALU.max), reads=[key_a, key_b], writes=[key_b])
            P.op("act", lambda: nc.scalar.activation(out=cos_ap, in_=tB, func=AF.Sin, scale=-1.0, bias=halfpi[:, 0:1]),
                 reads=[key_b, "halfpi"], writes=[key_c])

        DBG = set(os.environ.get("KDBG", "").split(",")) - {""}

        def dump(name, ap, shape, dt=F32):
            if name not in DBG:
                return
            t = nc.dram_tensor("dbg_" + name, list(shape), dt, kind="ExternalOutput").ap()
            P.barrier()
            P.dma("sp", "dbg_" + name, t, ap)
            P.barrier()

        def checkpoint(n, stacks=()):
            if n > KSTOP:
                P.barrier()
                for s_ in stacks:
                    s_.close()
                raise _Stop()

        def _layer(l):
            lam_init = 0.8 - 0.6 * math.exp(-0.3 * l)
            L = "L%d" % l
            w_in_l = p["w_in"][l]

            checkpoint(10 * l + 0)
            rmsnorm_to_hT(p["norm_mix"][l], es, "a" + L)

            if l == 0:
                dump("hT0", hT[:], [128, 8, S], BF16)
            checkpoint(10 * l + 1)
            with ExitStack() as st:
                wus = [sb("wu%d%s" % (i, L), [128, 8, 128], BF16, st) for i in range(1)]
                lr2 = sb("lr2" + L, [128, G], F32, st)
                li2 = sb("li2" + L, [128, G], F32, st)
                stp = sb("stp" + L, [128, G], F32, st)
                Cs = sb("Cs" + L, [128, G, 16], F32, st)
                Csw = sb("Csw" + L, [128, G, 16], F32, st)
                with ExitStack() as stl:
                    lraw = sb("lraw" + L, [32, 2, 128], F32, stl)
                    for half in range(2):
                        P.dma("sp", "lraw", lraw[:, 0, half * 64:(half + 1) * 64], p["lam_re"][l], writes=["lraw"])
                        P.dma("sp", "lraw", lraw[:, 1, half * 64:(half + 1) * 64], p["lam_im"][l], writes=["lraw"])
                    for i_, dst_ in enumerate((lr2, li2)):
                        P.op("pe", lambda i_=i_: nc.tensor.transpose(ps[0][:, i_ * 32:(i_ + 1) * 32], lraw[:, i_, :], ident_f[0:32, 0:32]),
                             reads=["lraw", "ident_f"], writes=[PK(0)])
                    P.op("dve", lambda: nc.vector.tensor_copy(out=lr2[:], in_=ps[0][:, 0:32]), reads=[PK(0)], writes=["lr2"])
                    P.op("dve", lambda: nc.vector.tensor_copy(out=li2[:], in_=ps[0][:, 32:64]), reads=[PK(0)], writes=["li2"])
                    craw = sb("craw" + L, [128, 4, 2, 128], F32, stl)
                    c_re_v = p["c_re"][l].rearrange("g c p -> (g c) p")
                    c_im_v = p["c_im"][l].rearrange("g c p -> (g c) p")
                    for rt in range(4):
                        rs_ = slice(rt * 128, (rt + 1) * 128)
                        P.dma("sp", "craw", craw[:, rt, 0, 0:64], c_re_v[rs_, :], writes=["craw"])
                        P.dma("sp", "craw", craw[:, rt, 0, 64:128], c_im_v[rs_, :], writes=["craw"])
                        P.dma("sp", "craw", craw[:, rt, 1, 0:64], c_im_v[rs_, :], writes=["craw"])
                        P.dma("sp", "craw", craw[:, rt, 1, 64:128], c_re_v[rs_, :], writes=["craw"])
                    for w_, dst_, dk in ((0, Cs, "Cs"), (1, Csw, "Csw")):
                        for rt in range(4):
                            P.op("pe", lambda rt=rt, w_=w_: nc.tensor.transpose(ps[1 + w_][:, rt * 128:(rt + 1) * 128], craw[:, rt, w_, :], ident_f[:]),
                                 reads=["craw", "ident_f"], writes=[PK(1 + w_)])
                        P.op("dve", lambda w_=w_, dst_=dst_: nc.vector.tensor_copy(out=dst_[:].rearrange("p g c -> p (g c)"), in_=ps[1 + w_][:, :]),
                             reads=[PK(1 + w_)], writes=[dk])
                    P.barrier()
                checkpoint(10 * l + 1.1, [st])
                P.dma("sp", "stp", stp[:], p["log_step"][l:l + 1, :].partition_broadcast(128), writes=["stp"])
                d16 = sb("d16" + L, [128, G], F32, st)
                for i in range(8):
                    P.dma("sp", "d16", d16[i * 16:(i + 1) * 16, :], p["d_skip"][l].rearrange("(g c) -> c g", c=16), writes=["d16"])

                ls = sb("ls" + L, [128, G], F32, st)
                th = sb("th" + L, [128, G], F32, st)
                t0 = sb("t0" + L, [128, G], F32, st)
                t1 = sb("t1" + L, [128, G], F32, st)
                t2 = sb("t2" + L, [128, G], F32, st)
                t3 = sb("t3" + L, [128, G], F32, st)
                t4 = sb("t4" + L, [128, G], F32, st)
                rho1 = sb("rho1" + L, [128, G], F32, st)
                rho8 = sb("rho8" + L, [128, G], F32, st)
                phi = sb("phi" + L, [128, G], F32, st)
                cn = sb("cn" + L, [128, G], F32, st)
                sn = sb("sn" + L, [128, G], F32, st)
                nr = sb("nr" + L, [128, G], F32, st)
                ni = sb("ni" + L, [128, G], F32, st)
                den = sb("den" + L, [128, G], F32, st)
                fre = sb("fre" + L, [128, G], F32, st)
                fim = sb("fim" + L, [128, G], F32, st)
                FB = sb("FB" + L, [128, G], F32, st)
                FD = sb("FD" + L, [128, G], F32, st)
                Bs = sb("Bs" + L, [128, G, 16], F32, st)
                Bsw = sb("Bsw" + L, [128, G, 16], F32, st)
                rW = sb("rW" + L, [128, G, 8], F32, st)
                rV = sb("rV" + L, [128, G, 8], F32, st)
                rVn = sb("rVn" + L, [128, G, 8], F32, st)
                csn = sb("csn" + L, [128, G, 8], F32, st)
                snn = sb("snn" + L, [128, G, 8], F32, st)

                st2 = ExitStack()
                braw = sb("braw" + L, [128, G, 16], F32, st2)
                brsw = sb("brsw" + L, [128, G, 16], F32, st2)
                P.dma("sp", "braw", braw[0:64], p["b_re"][l].rearrange("g p c -> p g c"), writes=["braw"])
                P.dma("sp", "braw", braw[64:128], p["b_im"][l].rearrange("g p c -> p g c"), writes=["braw"])
                P.dma("sp", "brsw", brsw[0:64], p["b_im"][l].rearrange("g p c -> p g c"), writes=["brsw"])
                P.dma("sp", "brsw", brsw[64:128], p["b_re"][l].rearrange("g p c -> p g c"), writes=["brsw"])
                tB = sb("tB" + L, [128, G, 16], F32, st2)
                def V(fn, r, w):
                    return P.op("dve", fn, reads=r, writes=w)

                def Gp(fn, r, w):
                    return P.op("pool", fn, reads=r, writes=w)

                checkpoint(10 * l + 1.2, [st2, st])
                P.op("act", lambda: nc.scalar.activation(out=stp[:], in_=stp[:], func=AF.Exp), reads=["stp"], writes=["stp"])
                V(lambda: nc.vector.tensor_tensor(out=ls[:], in0=lr2[:], in1=stp[:], op=ALU.mult), ["lr2", "stp"], ["ls"])
                V(lambda: nc.vector.tensor_tensor(out=th[:], in0=li2[:], in1=stp[:], op=ALU.mult), ["li2", "stp"], ["th"])
                expsmall(rho1[:], ls[:], t0[:], ["ls"], "rho1", "t0")
                V(lambda: nc.vector.tensor_scalar(out=t1[:], in0=ls[:], scalar1=8.0, scalar2=None, op0=ALU.mult), ["ls"], ["t1"])
                expsmall(rho8[:], t1[:], t0[:], ["t1"], "rho8", "t0")
                sincos_pos(sn[:], cn[:], th[:], t2[:], t4[:], 1.0, TWO_PI, ["th"], "sn", "cn", "t2", "t4")
                sin_pos(phi[:], th[:], t0[:], t3[:], 1.0, TWO_PI, ["th"], "phi", "t0", "t3", do_sin=False)
                sin_pos(phi[:], phi[:], t0[:], t3[:], 8.0, 5 * TWO_PI, ["phi"], "phi", "t0", "t3", do_sin=False)
                V(lambda: nc.vector.tensor_scalar(out=t0[:], in0=phi[:], scalar1=0.0, scalar2=TWO_PI, op0=ALU.is_lt, op1=ALU.mult), ["phi"], ["t0"])
                V(lambda: nc.vector.tensor_tensor(out=phi[:], in0=phi[:], in1=t0[:], op=ALU.add), ["phi", "t0"], ["phi"])
                checkpoint(10 * l + 1.3, [st2, st])
                V(lambda: nc.vector.tensor_tensor(out=nr[:], in0=rho1[:], in1=cn[:], op=ALU.mult), ["rho1", "cn"], ["nr"])
                V(lambda: nc.vector.tensor_scalar(out=nr[:], in0=nr[:], scalar1=-1.0, scalar2=None, op0=ALU.add),
                  ["nr"], ["nr"])
                V(lambda: nc.vector.tensor_tensor(out=ni[:], in0=rho1[:], in1=sn[:], op=ALU.mult), ["rho1", "sn"], ["ni"])
                V(lambda: nc.vector.tensor_tensor(out=den[:], in0=lr2[:], in1=lr2[:], op=ALU.mult), ["lr2"], ["den"])
                V(lambda: nc.vector.tensor_tensor(out=t1[:], in0=li2[:], in1=li2[:], op=ALU.mult), ["li2"], ["t1"])
                V(lambda: nc.vector.tensor_tensor(out=den[:], in0=den[:], in1=t1[:], op=ALU.add), ["den", "t1"], ["den"])
                V(lambda: nc.vector.reciprocal(out=den[:], in_=den[:]), ["den"], ["den"])
                V(lambda: nc.vector.tensor_tensor(out=fre[:], in0=nr[:], in1=lr2[:], op=ALU.mult), ["nr", "lr2"], ["fre"])
                V(lambda: nc.vector.tensor_tensor(out=t1[:], in0=ni[:], in1=li2[:], op=ALU.mult), ["ni", "li2"], ["t1"])
                V(lambda: nc.vector.tensor_tensor(out=fre[:], in0=fre[:], in1=t1[:], op=ALU.add), ["fre", "t1"], ["fre"])
                V(lambda: nc.vector.tensor_tensor(out=fre[:], in0=fre[:], in1=den[:], op=ALU.mult), ["fre", "den"], ["fre"])
                V(lambda: nc.vector.tensor_tensor(out=fim[:], in0=ni[:], in1=lr2[:], op=ALU.mult), ["ni", "lr2"], ["fim"])
                V(lambda: nc.vector.tensor_tensor(out=t1[:], in0=nr[:], in1=li2[:], op=ALU.mult), ["nr", "li2"], ["t1"])
                V(lambda: nc.vector.tensor_tensor(out=fim[:], in0=fim[:], in1=t1[:], op=ALU.subtract), ["fim", "t1"], ["fim"])
                V(lambda: nc.vector.tensor_tensor(out=fim[:], in0=fim[:], in1=den[:], op=ALU.mult), ["fim", "den"], ["fim"])
                V(lambda: nc.vector.tensor_scalar(out=FB[:], in0=fim[:], scalar1=sgn[:, 0:1], scalar2=None, op0=ALU.mult),
                  ["fim", "sgn"], ["FB"])
                V(lambda: nc.vector.tensor_scalar(out=FD[:], in0=fre[:], scalar1=sgn[:, 1:2], scalar2=None, op0=ALU.mult),
                  ["fre", "sgn"], ["FD"])

                checkpoint(10 * l + 1.4, [st2, st])

                def bc16(t):
                    return t[:, :].unsqueeze(2).broadcast_to([128, G, 16])

                V(lambda: nc.vector.tensor_tensor(out=Bs[:], in0=braw[:], in1=bc16(fre), op=ALU.mult), ["braw", "fre"], ["Bs"])
                V(lambda: nc.vector.tensor_tensor(out=tB[:], in0=brsw[:], in1=bc16(FB), op=ALU.mult), ["brsw", "FB"], ["tB"])
                V(lambda: nc.vector.tensor_tensor(out=Bs[:], in0=Bs[:], in1=tB[:], op=ALU.add), ["Bs", "tB"], ["Bs"])
                V(lambda: nc.vector.tensor_tensor(out=Bsw[:], in0=braw[:], in1=bc16(fim), op=ALU.mult), ["braw", "fim"], ["Bsw"])
                V(lambda: nc.vector.tensor_tensor(out=tB[:], in0=brsw[:], in1=bc16(FD), op=ALU.mult), ["brsw", "FD"], ["tB"])
                V(lambda: nc.vector.tensor_tensor(out=Bsw[:], in0=Bsw[:], in1=tB[:], op=ALU.add), ["Bsw", "tB"], ["Bsw"])
                checkpoint(10 * l + 1.5, [st2, st])
                P.barrier()
                st2.close()
                with ExitStack() as stv:
                    sth = sb("sth" + L, [128, G, 8], F32, stv)
                    tvA = sb("tvA" + L, [128, G, 8], F32, stv)
                    tvB = sb("tvB" + L, [128, G, 8], F32, stv)
                    sidx = jidx[:, 0:8].unsqueeze(1).broadcast_to([128, G, 8])
                    V(lambda: nc.vector.tensor_tensor(out=sth[:], in0=ls[:, :].unsqueeze(2).broadcast_to([128, G, 8]), in1=sidx, op=ALU.mult),
                      ["ls", "jidx"], ["sth"])
                    fl3 = lambda t: t[:].rearrange("p g s -> p (g s)")
                    P.op("act", lambda: nc.scalar.activation(out=fl3(rW), in_=fl3(sth), func=AF.Exp, scale=-1.0), reads=["sth"], writes=["rW"])
                    P.op("act", lambda: nc.scalar.activation(out=fl3(rV), in_=fl3(sth), func=AF.Exp, scale=1.0), reads=["sth"], writes=["rV"])
                    V(lambda: nc.vector.tensor_tensor(out=sth[:], in0=th[:, :].unsqueeze(2).broadcast_to([128, G, 8]), in1=sidx, op=ALU.mult),
                      ["th", "jidx"], ["sth"])
                    sincos_pos(fl3(snn), fl3(csn), fl3(sth), fl3(tvA), fl3(tvB), 1.0, TWO_PI, ["sth"], "snn", "csn", "tvA", "tvB")
                    P.barrier()

                V(lambda: nc.vector.tensor_scalar(out=Cs[:], in0=Cs[:], scalar1=sgn[:, 1:2], scalar2=None, op0=ALU.mult), ["Cs", "sgn"], ["Cs"])
                V(lambda: nc.vector.tensor_tensor(out=rVn[:], in0=csn[:], in1=rV[:], op=ALU.mult), ["csn", "rV"], ["rVn"])
                V(lambda: nc.vector.tensor_tensor(out=rV[:], in0=snn[:], in1=rV[:], op=ALU.mult), ["snn", "rV"], ["rV"])
                V(lambda: nc.vector.tensor_tensor(out=csn[:], in0=csn[:], in1=rW[:], op=ALU.mult), ["csn", "rW"], ["csn"])
                V(lambda: nc.vector.tensor_tensor(out=snn[:], in0=snn[:], in1=rW[:], op=ALU.mult), ["snn", "rW"], ["snn"])
                checkpoint(10 * l + 2, [st])
                U8b = sb("U8b" + L, [128, 2, 8, 8, 16], F32, st)
                Ub = sb("Ub" + L, [128, 8, 256], F32, st)
                S8b = sb("S8b" + L, [128, 2, 8, 128], BF16, st)
                WnS = [sb("Wn%d%s" % (i, L), [128, 4, 8, 16], F32, st) for i in range(2)]
                WswnS = [sb("Wswn%d%s" % (i, L), [128, 4, 8, 16], F32, st) for i in range(2)]
                VnS = [sb("Vn%d%s" % (i, L), [128, 4, 8, 16], F32, st) for i in range(2)]
                VswnS = [sb("Vswn%d%s" % (i, L), [128, 4, 8, 16], F32, st) for i in range(2)]
                tP = sb("tP" + L, [128, 4, 8, 16], F32, st)
                WnT = sb("WnT" + L, [128, 4, 128], F32, st)
                WswnT = sb("WswnT" + L, [128, 4, 128], F32, st)
                Tp = sb("Tp" + L, [128, 4, 128], F32, st)
                COSn = sb("COSn" + L, [128, 4, 256], F32, st)
                SINn = sb("SINn" + L, [128, 4, 256], F32, st)
                RHO = sb("RHO" + L, [128, 4, 256], F32, st)
                vin = sb("vin" + L, [128, 4, 256], F32, st)
                vv = sb("vv" + L, [128, 4, 256], F32, st)
                cv = sb("cv" + L, [128, 4, 256], F32, st)
                Sg = sb("Sg" + L, [128, 8, 256], BF16, st)

                for b in range(4):
                    wu = wus[0]
                    P.dma("pool", "wu0", wu[:, :, :],
                          w_in_l[:, 1536 + b * 128:1536 + (b + 1) * 128].rearrange("(kt p) c -> p kt c", p=128), writes=[("wu", 0)])
                    for half in range(2):
                        pb = half
                        for q in range(8):
                            tk = half * 8 + q
                            jt, s_ = tk // 8, tk % 8
                            t_start = jt * 1024 + s_
                            for kt in range(8):
                                P.op("pe", lambda kt=kt, q=q, pb=pb, t_start=t_start: nc.tensor.matmul(
                                    ps[pb][:, q * 64:(q + 1) * 64],
                                    lhsT=hT[:, kt, t_start:t_start + 1017:8], rhs=wu[:, kt, 0:64],
                                    start=(kt == 0), stop=(kt == 7)),
                                    reads=[("hT", 0), ("hT", 1), ("hT", 2), ("hT", 3), ("wu", 0)], writes=[PK(pb)], signal=(kt == 7))
                    for half in range(2):
                        pb = 2 + half
                        for q in range(8):
                            tk = half * 8 + q
                            jt, s_ = tk // 8, tk % 8
                            t_start = jt * 1024 + s_
                            for kt in range(8):
                                P.op("pe", lambda kt=kt, q=q, pb=pb, t_start=t_start: nc.tensor.matmul(
                                    ps[pb][:, q * 64:(q + 1) * 64],
                                    lhsT=hT[:, kt, t_start:t_start + 1017:8], rhs=wu[:, kt, 64:128],
                                    start=(kt == 0), stop=(kt == 7)),
                                    reads=[("hT", 0), ("hT", 1), ("hT", 2), ("hT", 3), ("wu", 0)], writes=[PK(pb)], signal=(kt == 7))
                    for half in range(2):
                        for cc in range(2):
                            pb = cc * 2 + half
                            outap = U8b[:, half, cc * 4:(cc + 1) * 4, :, :].rearrange("p g s c -> p s g c")
                            inap = ps[pb][:, :].rearrange("p (s g c) -> p s g c", s=8, g=4)
                            if cc == 0:
                                P.op("dve", lambda outap=outap, inap=inap: nc.vector.tensor_copy(out=outap, in_=inap),
                                     reads=[PK(pb)], writes=["U8b"])
                            else:
                                P.op("act", lambda outap=outap, inap=inap: nc.scalar.copy(out=outap, in_=inap),
                                     reads=[PK(pb)], writes=["U8b"])
                    for gl in range(8):
                        pb = 4 + (gl // 2) % 2
                        for jt in range(2):
                            col = ((gl % 2) * 2 + jt) * 128
                            P.op("pe", lambda gl=gl, jt=jt, pb=pb, col=col: nc.tensor.transpose(
                                ps[pb][:, col:col + 128], U8b[:, jt, gl, :, :].rearrange("p s c -> p (s c)"), ident_f[:]),
                                reads=["U8b", "ident_f"], writes=[PK(pb)])
                        if gl % 2 == 1:
                            outap = Ub[:, gl - 1:gl + 1, :]
                            inap = ps[pb][:, :].rearrange("p (g j) -> p g j", g=2)
                            P.op("dve", lambda outap=outap, inap=inap: nc.vector.tensor_copy(out=outap, in_=inap),
                                 reads=[PK(pb)], writes=[("Ub", gl // 2)])
                    for hb in range(2):
                        g0 = b * 8 + hb * 4
                        si = (b * 2 + hb) % 2
                        Wn, Wswn, Vn, Vswn = WnS[si], WswnS[si], VnS[si], VswnS[si]
                        kWn, kWswn, kVn, kVswn = "Wn%d" % si, "Wswn%d" % si, "Vn%d" % si, "Vswn%d" % si

                        def bcs(t):
                            return t[:, g0:g0 + 4, :].unsqueeze(3).broadcast_to([128, 4, 8, 16])

                        def bcg(t):
                            return t[:, g0:g0 + 4, :].unsqueeze(2).broadcast_to([128, 4, 8, 16])

                        Gp(lambda: nc.gpsimd.tensor_tensor(out=Wn[:], in0=bcs(csn), in1=bcg(Bs), op=ALU.mult), ["csn", "Bs"], [kWn])
                        Gp(lambda: nc.gpsimd.tensor_tensor(out=tP[:], in0=bcs(snn), in1=bcg(Bsw), op=ALU.mult), ["snn", "Bsw"], ["tP"])
                        Gp(lambda: nc.gpsimd.tensor_tensor(out=Wn[:], in0=Wn[:], in1=tP[:], op=ALU.add), [kWn, "tP"], [kWn])
                        Gp(lambda: nc.gpsimd.tensor_tensor(out=Wswn[:], in0=bcs(csn), in1=bcg(Bsw), op=ALU.mult), ["csn", "Bsw"], [kWswn])
                        Gp(lambda: nc.gpsimd.tensor_tensor(out=tP[:], in0=bcs(snn), in1=bcg(Bs), op=ALU.mult), ["snn", "Bs"], ["tP"])
                        Gp(lambda: nc.gpsimd.tensor_tensor(out=Wswn[:], in0=Wswn[:], in1=tP[:], op=ALU.subtract), [kWswn, "tP"], [kWswn])
                        Gp(lambda: nc.gpsimd.tensor_tensor(out=Vn[:], in0=bcs(rVn), in1=bcg(Cs), op=ALU.mult), ["rVn", "Cs"], [kVn])
                        Gp(lambda: nc.gpsimd.tensor_tensor(out=tP[:], in0=bcs(rV), in1=bcg(Csw), op=ALU.mult), ["rV", "Csw"], ["tP"])
                        Gp(lambda: nc.gpsimd.tensor_tensor(out=Vn[:], in0=Vn[:], in1=tP[:], op=ALU.subtract), [kVn, "tP"], [kVn])
                        Gp(lambda: nc.gpsimd.tensor_tensor(out=Vswn[:], in0=bcs(rV), in1=bcg(Cs), op=ALU.mult), ["rV", "Cs"], [kVswn])
                        Gp(lambda: nc.gpsimd.tensor_tensor(out=tP[:], in0=bcs(rVn), in1=bcg(Csw), op=ALU.mult), ["rVn", "Csw"], ["tP"])
                        Gp(lambda: nc.gpsimd.tensor_tensor(out=Vswn[:], in0=Vswn[:], in1=tP[:], op=ALU.add), [kVswn, "tP"], [kVswn])
                        Gp(lambda: nc.gpsimd.tensor_tensor(out=tP[:].rearrange("p g s c -> p g (s c)"), in0=ident_f[:, :].unsqueeze(1).broadcast_to([128, 4, 128]),
                                                           in1=d16[:, g0:g0 + 4].unsqueeze(2).broadcast_to([128, 4, 128]), op=ALU.mult),
                           ["ident_f", "d16"], ["tP"])

                        def flat(t, g_):
                            return t[:, g_, :, :].rearrange("p s c -> p (s c)")

                        for g_ in range(4):
                            P.op("pe", lambda g_=g_: nc.tensor.transpose(ps[6][:, g_ * 128:(g_ + 1) * 128], flat(Wn, g_), ident_f[:]),
                                 reads=[kWn, "ident_f"], writes=[PK(6)])
                        V(lambda: nc.vector.tensor_copy(out=WnT[:], in_=ps[6][:, :].rearrange("p (g m) -> p g m", g=4)), [PK(6)], ["WnT"])
                        for g_ in range(4):
                            P.op("pe", lambda g_=g_: nc.tensor.transpose(ps[7][:, g_ * 128:(g_ + 1) * 128], flat(Wswn, g_), ident_f[:]),
                                 reads=[kWswn, "ident_f"], writes=[PK(7)])
                        P.op("act", lambda: nc.scalar.copy(out=WswnT[:], in_=ps[7][:, :].rearrange("p (g m) -> p g m", g=4)),
                             reads=[PK(7)], writes=["WswnT"])
                        for g_ in range(4):
                            P.op("pe", lambda g_=g_: nc.tensor.matmul(ps[6][:, g_ * 128:(g_ + 1) * 128], lhsT=flat(Wn, g_), rhs=flat(Vn, g_),
                                                                     start=True, stop=True),
                                 reads=[kWn, kVn], writes=[PK(6)])
                        V(lambda: nc.vector.tensor_tensor(out=Tp[:], in0=ps[6][:, :].rearrange("p (g m) -> p g m", g=4),
                                                          in1=negmask[:, :].unsqueeze(1).broadcast_to([128, 4, 128]), op=ALU.mult),
                          [PK(6), "negmask"], ["Tp"])
                        V(lambda: nc.vector.tensor_tensor(out=Tp[:], in0=Tp[:], in1=tP[:].rearrange("p g s c -> p g (s c)"), op=ALU.add), ["Tp", "tP"], ["Tp"])
                        for g_ in range(4):
                            gl = hb * 4 + g_
                            P.op("pe", lambda g_=g_, gl=gl: nc.tensor.matmul(ps[g_ // 2][:, (g_ % 2) * 256:(g_ % 2 + 1) * 256], lhsT=WnT[:, g_, :],
                                                                             rhs=Ub[:, gl, :], start=True, stop=True),
                                 reads=["WnT", ("Ub", gl // 2)], writes=[PK(g_ // 2)])
                            P.op("pe", lambda g_=g_, gl=gl: nc.tensor.matmul(ps[2 + g_ // 2][:, (g_ % 2) * 256:(g_ % 2 + 1) * 256], lhsT=WswnT[:, g_, :],
                                                                             rhs=Ub[:, gl, :], start=True, stop=True),
                                 reads=["WswnT", ("Ub", gl // 2)], writes=[PK(2 + g_ // 2)])
                        V(lambda: nc.vector.tensor_tensor(out=cv[:], in0=phi[:, g0:g0 + 4].unsqueeze(2).broadcast_to([128, 4, 256]),
                                                          in1=jidx[:, :].unsqueeze(1).broadcast_to([128, 4, 256]), op=ALU.mult),
                          ["phi", "jidx"], ["cv"])
                        fl = lambda t: t[:].rearrange("p g j -> p (g j)")
                        sincos_pos(fl(SINn), fl(COSn), fl(cv), fl(vin), fl(vv), 1.0, 0.0, ["cv"], "SINn", "COSn", "vin", "vv")
                        V(lambda: nc.vector.tensor_tensor(out=RHO[:], in0=rho8[:, g0:g0 + 4].unsqueeze(2).broadcast_to([128, 4, 256]),
                                                          in1=jmask[:, :].unsqueeze(1).broadcast_to([128, 4, 256]), op=ALU.mult),
                          ["rho8", "jmask"], ["RHO"])
                        for hh in range(2):
                            V(lambda hh=hh: nc.vector.tensor_tensor(out=vin[:, hh * 2:hh * 2 + 2, :], in0=ps[hh][:, :].rearrange("p (g j) -> p g j", g=2),
                                                                    in1=COSn[:, hh * 2:hh * 2 + 2, :], op=ALU.mult),
                              [PK(hh), "COSn", "SINn"], ["vin"])
                            V(lambda hh=hh: nc.vector.tensor_tensor(out=cv[:, hh * 2:hh * 2 + 2, :], in0=ps[2 + hh][:, :].rearrange("p (g j) -> p g j", g=2),
                                                                    in1=SINn[:, hh * 2:hh * 2 + 2, :], op=ALU.mult),
                              [PK(2 + hh), "SINn"], ["cv"])
                        V(lambda: nc.vector.tensor_tensor(out=vin[:], in0=vin[:], in1=cv[:], op=ALU.add), ["vin", "cv"], ["vin"])
                        V(lambda: nc.vector.tensor_tensor_scan(out=vv[:].rearrange("p g j -> p (g j)"), data0=RHO[:].rearrange("p g j -> p (g j)"),
                                                               data1=vin[:].rearrange("p g j -> p (g j)"), initial=0.0,
                                                               op0=ALU.mult, op1=ALU.add),
                          ["RHO", "vin"], ["vv"])
                        V(lambda: nc.vector.tensor_tensor(out=cv[:], in0=COSn[:], in1=vv[:], op=ALU.mult), ["COSn", "vv"], ["cv"])
                        V(lambda: nc.vector.scalar_tensor_tensor(out=vin[:], in0=SINn[:], scalar=-1.0, in1=vv[:], op0=ALU.mult, op1=ALU.mult), ["SINn", "vv"], ["vin"])
                        for g_ in range(4):
                            gl = hb * 4 + g_
                            o_ = ps[4 + g_ // 2][:, (g_ % 2) * 256:(g_ % 2 + 1) * 256]
                            P.op("pe", lambda g_=g_, gl=gl, o_=o_: nc.tensor.matmul(o_, lhsT=Tp[:, g_, :], rhs=Ub[:, gl, :], start=True, stop=False),
                                 reads=["Tp", ("Ub", gl // 2)], writes=[PK(4 + g_ // 2)], signal=False)
                            P.op("pe", lambda g_=g_, o_=o_: nc.tensor.matmul(o_, lhsT=flat(Vn, g_), rhs=cv[:, g_, :], start=False, stop=False),
                                 reads=[kVn, "cv"], writes=[PK(4 + g_ // 2)], signal=False)
                            P.op("pe", lambda g_=g_, o_=o_: nc.tensor.matmul(o_, lhsT=flat(Vswn, g_), rhs=vin[:, g_, :], start=False, stop=True),
                                 reads=[kVswn, "vin"], writes=[PK(4 + g_ // 2)])
                        for hh in range(2):
                            P.op("act", lambda hh=hh: nc.scalar.activation(out=Sg[:, hb * 4 + hh * 2:hb * 4 + hh * 2 + 2, :],
                                                                           in_=ps[4 + hh][:, :].rearrange("p (g j) -> p g j", g=2),
                                                                           func=AF.Gelu_apprx_tanh),
                                 reads=[PK(4 + hh)], writes=["Sg"])
                    for jt in range(2):
                        pb = 6 + jt
                        psb = ps[pb][:, :].bitcast(BF16)
                        for gl in range(8):
                            P.op("pe", lambda gl=gl, jt=jt, psb=psb: nc.tensor.transpose(psb[:, gl * 128:(gl + 1) * 128],
                                                                                         Sg[:, gl, jt * 128:(jt + 1) * 128], ident_b[:]),
                                 reads=["Sg", "ident_b"], writes=[PK(pb)])
                        outap = S8b[:, jt, :, :].rearrange("p i (g c) -> p g i c", g=8)
                        inap = psb.rearrange("p (g i c) -> p g i c", g=8, i=8)
                        P.op("dve", lambda outap=outap, inap=inap: nc.vector.tensor_copy(out=outap, in_=inap), reads=[PK(pb)], writes=["S8b"])
                    for jt in range(2):
                        pb = 6 + jt
                        psb = ps[pb][:, :].bitcast(BF16)
                        for i in range(8):
                            P.op("pe", lambda i=i, jt=jt, psb=psb: nc.tensor.transpose(psb[:, i * 128:(i + 1) * 128], S8b[:, jt, i, :], ident_b[:]),
                                 reads=["S8b", "ident_b"], writes=[PK(pb)])
                        outap = sT[:, b, jt * 1024:(jt + 1) * 1024].rearrange("p (j i) -> p j i", i=8)
                        inap = psb.rearrange("p (i j) -> p j i", i=8)
                        P.op("act", lambda outap=outap, inap=inap: nc.scalar.copy(out=outap, in_=inap), reads=[PK(pb)], writes=[("sT", "all")])
                P.barrier()

            if l == 0:
                dump("sT0", sT[:], [128, 4, S], BF16)
            checkpoint(10 * l + 3)
            with ExitStack() as st:
                wg = sb("wglu" + L, [128, 4, 512], BF16, st)
                for c in range(2):
                    P.dma("pool", "wglu%d" % c, wg[:, :, c * 256:(c + 1) * 256],
                          p["w_glu"][l][:, c * 256:(c + 1) * 256].rearrange("(kt p) c -> p kt c", p=128), writes=[("wglu", c)])
                sig = [sb("gsig%d%s" % (i, L), [128, CH], BF16, st) for i in range(2)]
                for c in range(NCH):
                    cs = slice(c * CH, (c + 1) * CH)
                    for mt in range(4):
                        for kt in range(4):
                            P.op("pe", lambda mt=mt, kt=kt, cs=cs: nc.tensor.matmul(ps[mt][:, :], lhsT=wg[:, kt, mt * 128:(mt + 1) * 128],
                                                                                   rhs=sT[:, kt, cs], start=(kt == 0), stop=(kt == 3)),
                                 reads=[("wglu", 0), ("wglu", 1), ("sT", "all")], writes=[PK(mt)], signal=(kt == 3))
                    for mt in range(4):
                        sg_ = sig[mt % 2]
                        P.op("act", lambda mt=mt, sg_=sg_: nc.scalar.activation(out=sg_[:], in_=ps[mt][:, :], func=AF.Sigmoid),
                             reads=[PK(mt)], writes=[("gsig", mt % 2)])
                        P.op("dve", lambda mt=mt, sg_=sg_, cs=cs: nc.vector.tensor_tensor(out=sT[:, mt, cs], in0=sT[:, mt, cs], in1=sg_[:], op=ALU.mult),
                             reads=[("gsig", mt % 2), ("sT", "all")], writes=[("sT", "all")])
                P.barrier()

            checkpoint(10 * l + 4)
            with ExitStack() as stB:
                oT = sb("oT" + L, [128, 4, S], BF16, stB)
                with ExitStack() as st:
                    qT = sb("qT" + L, [128, 4, S], BF16, st)
                    kT = sb("kT" + L, [128, 4, S], BF16, st)
                    Vt = sb("Vt" + L, [128, 16, 512], BF16, st)
                    gains = sb("gains" + L, [128, 2], F32, st)
                    for half in range(2):
                        P.dma("sp", "gains", gains[half * 64:(half + 1) * 64, 0:1], p["q_gain"][l].rearrange("(p o) -> p o", o=1), writes=["gains"])
                        P.dma("sp", "gains", gains[half * 64:(half + 1) * 64, 1:2], p["k_gain"][l].rearrange("(p o) -> p o", o=1), writes=["gains"])
                    lamv = sb("lamv" + L, [128, 4, 64], F32, st)
                    for i_, nm_ in enumerate(("lambda_q1", "lambda_k1", "lambda_q2", "lambda_k2")):
                        P.dma("sp", "lamv", lamv[:, i_, :], p[nm_][l:l + 1, :].partition_broadcast(128), writes=["lamv"])
                    lam2 = sb("lam2" + L, [128, 2, 64], F32, st)
                    lsum = sb("lsum" + L, [128, 2], F32, st)
                    neglam = sb("neglam" + L, [128, 1], F32, st)
                    sublw = sb("sublw" + L, [128, 1], F32, st)
                    P.dma("sp", "sublw", sublw[:], p["subln"][l].rearrange("(p o) -> p o", o=1), writes=["sublw"])
                    P.op("dve", lambda: nc.vector.tensor_tensor(out=lam2[:, 0, :], in0=lamv[:, 0, :], in1=lamv[:, 1, :], op=ALU.mult),
                         reads=["lamv"], writes=["lam2"])
                    P.op("dve", lambda: nc.vector.tensor_tensor(out=lam2[:, 1, :], in0=lamv[:, 2, :], in1=lamv[:, 3, :], op=ALU.mult),
                         reads=["lamv"], writes=["lam2"])
                    P.op("dve", lambda: nc.vector.reduce_sum(out=lsum[:], in_=lam2[:], axis=AX.X), reads=["lam2"], writes=["lsum"])
                    P.op("act", lambda: nc.scalar.activation(out=lsum[:], in_=lsum[:], func=AF.Exp), reads=["lsum"], writes=["lsum"])
                    P.op("dve", lambda: nc.vector.tensor_tensor(out=neglam[:], in0=lsum[:, 1:2], in1=lsum[:, 0:1], op=ALU.subtract),
                         reads=["lsum"], writes=["neglam"])
                    P.op("dve", lambda: nc.vector.tensor_scalar(out=neglam[:], in0=neglam[:], scalar1=-lam_init, scalar2=None, op0=ALU.add),
                         reads=["neglam"], writes=["neglam"])
                    P.op("dve", lambda: nc.vector.tensor_scalar(out=sublw[:], in0=sublw[:], scalar1=1.0 - lam_init, scalar2=None, op0=ALU.mult),
                         reads=["sublw"], writes=["sublw"])

                    stp_ = ExitStack()
                    wsl = [sb("wsA%d%s" % (i, L), [128, 8, 256], BF16, stp_) for i in range(2)]
                    wkeys = ["wsA%d" % i for i in range(2)]
                    NQB = 3
                    raw = [sb("qraw%d%s" % (i, L), [128, CH], F32, stp_) for i in range(NQB)]
                    sqb = [sb("qsq%d%s" % (i, L), [128, CH], BF16, stp_) for i in range(NQB)]
                    rs = [sb("qrs%d%s" % (i, L), [128, CH], F32, stp_) for i in range(NQB)]
                    PA_B = [0, 1, 4]
                    PN_B = [2, 3, 5]
                    wst = {"i": 0}
                    it = 0
                    for which in range(2):
                        dstT = qT if which == 0 else kT
                        for cc in range(2):
                            wt, wk = load_w(wsl, wkeys, w_in_l[:, which * 512 + cc * 256: which * 512 + (cc + 1) * 256], 8, 256, wst)
                            for hh in range(2):
                                h = cc * 2 + hh
                                for c in range(NCH):
                                    cs = slice(c * CH, (c + 1) * CH)
                                    qi = it % NQB
                                    pa = PA_B[qi]
                                    pn = PN_B[qi]
                                    it += 1
                                    for kt in range(8):
                                        P.op("pe", lambda kt=kt, wt=wt, hh=hh, cs=cs, pa=pa: nc.tensor.matmul(
                                            ps[pa][:, :], lhsT=wt[:, kt, hh * 128:(hh + 1) * 128], rhs=hT[:, kt, cs], start=(kt == 0), stop=(kt == 7)),
                                            reads=[wk, ("hT", c)], writes=[PK(pa)], signal=(kt == 7))
                                    P.op("act", lambda pa=pa, qi=qi: nc.scalar.activation(out=sqb[qi][:], in_=ps[pa][:, :], func=AF.Square),
                                         reads=[PK(pa)], writes=[("qsq", qi)])
                                    P.op("act", lambda pa=pa, qi=qi, which=which: nc.scalar.activation(out=raw[qi][:], in_=ps[pa][:, :], func=AF.Copy,
                                                                                                       scale=gains[:, which:which + 1]),
                                         reads=[PK(pa), "gains"], writes=[("qraw", qi)])
                                    P.op("pe", lambda qi=qi, pn=pn: nc.tensor.matmul(ps[pn][:, :], lhsT=blk_b[:], rhs=sqb[qi][:], start=True, stop=True),
                                         reads=[("qsq", qi), "blk_b"], writes=[PK(pn)])
                                    P.op("act", lambda qi=qi, pn=pn: nc.scalar.activation(out=rs[qi][:], in_=ps[pn][:, :], func=AF.Sqrt, scale=1.0 / 64, bias=epsb[:, 0:1]),
                                         reads=[PK(pn), "epsb"], writes=[("qrs", qi)])
                                    P.op("dve", lambda qi=qi: nc.vector.reciprocal(out=rs[qi][:], in_=rs[qi][:]),
                                         reads=[("qrs", qi)], writes=[("qrs", qi)])
                                    P.op("pool", lambda qi=qi, h=h, cs=cs, dstT=dstT: nc.gpsimd.tensor_tensor(
                                        out=dstT[:, h, cs], in0=raw[qi][:], in1=rs[qi][:], op=ALU.mult),
                                        reads=[("qraw", qi), ("qrs", qi)], writes=[("qk", which, h, c)])
                    wv = []
                    for cc in range(2):
                        wv.append(load_w(wsl, wkeys, w_in_l[:, 1024 + cc * 256:1024 + (cc + 1) * 256], 8, 256, wst))
                    for tt in range(16):
                        pa = tt % 2
                        for cc in range(2):
                            wt, wk = wv[cc]
                            for kt in range(8):
                                P.op("pe", lambda kt=kt, wt=wt, cc=cc, tt=tt, pa=pa: nc.tensor.matmul(
                                    ps[pa][:, cc * 256:(cc + 1) * 256], lhsT=hT[:, kt, tt * 128:(tt + 1) * 128], rhs=wt[:, kt, :],
                                    start=(kt == 0), stop=(kt == 7)),
                                    reads=[wk, ("hT", tt // 4)], writes=[PK(pa)], signal=(kt == 7))
                        if tt % 2 == 0:
                            P.op("act", lambda tt=tt, pa=pa: nc.scalar.copy(out=Vt[:, tt, :], in_=ps[pa][:, :]), reads=[PK(pa)], writes=[("Vt", tt)])
                        else:
                            P.op("dve", lambda tt=tt, pa=pa: nc.vector.tensor_copy(out=Vt[:, tt, :], in_=ps[pa][:, :]), reads=[PK(pa)], writes=[("Vt", tt)])

                    P.barrier()
                    stp_.close()
                    checkpoint(10 * l + 5, [st, stB])
                    hank = sb("hank" + L, [128, NH, HW_], BF16, st)
                    for h in range(NH):
                        src = bass.AP(tensor=tb_scr, offset=h * (HW_ + 128), ap=[[1, 128], [1, HW_]])
                        P.dma("sp", "hank", hank[:, h, :], src, writes=["hank"])
                    pt = [sb("pt%d%s" % (i, L), [128, CH], BF16, st) for i in range(4)]
                    fa = sb("fa" + L, [128, CH], F32, st)
                    fb = sb("fb" + L, [128, CH], F32, st)
                    fo = sb("fo" + L, [128, CH], F32, st)
                    fr = sb("fr" + L, [128, CH], F32, st)
                    fsq = sb("fsq" + L, [128, CH], BF16, st)
                    SCB = [0, 1, 7] if os.environ.get("KSCB", "3") == "3" else [0, 1]
                    state = {"pti": 0, "sci": 0}
                    deferred = []

                    def emit_qk(h, Q, m, kt):
                        r = kt - 4 * Q
                        c0 = 128 * r if r > 0 else 0
                        N = CH - c0
                        off = CH * Q + c0 - 128 * kt
                        near = off < 256
                        sb_ = SCB[state["sci"] % len(SCB)]
                        state["sci"] += 1
                        pi_ = state["pti"] % 4
                        state["pti"] += 1
                        p_ = pt[pi_]
                        pk_ = ("pt", pi_)
                        ms = slice(m * 64, (m + 1) * 64)
                        P.op("pe", lambda: nc.tensor.matmul(
                            ps[sb_][:, 0:N], lhsT=kT[ms, h, kt * 128:(kt + 1) * 128], rhs=qT[ms, h, Q * CH + c0:(Q + 1) * CH],
                            start=True, stop=(not near)),
                            reads=[("qk", 1, h, kt // 4), ("qk", 0, h, Q)], writes=[PK(sb_)], signal=(not near))
                        if near:
                            P.op("pe", lambda: nc.tensor.matmul(
                                ps[sb_][:, 0:N], lhsT=antiid_b[:], rhs=hank[:, h, off:off + N], start=False, stop=True),
                                reads=["antiid_b", "hank"], writes=[PK(sb_)])
                            P.op("act", lambda: nc.scalar.activation(out=p_[:, 0:N], in_=ps[sb_][:, 0:N], func=AF.Exp, scale=0.125),
                                 reads=[PK(sb_)], writes=[pk_])
                        else:
                            P.op("act", lambda: nc.scalar.activation(out=p_[:, 0:N], in_=ps[sb_][:, 0:N], func=AF.Exp,
                                                                     scale=0.125, bias=cbias[:, h:h + 1]),
                                 reads=[PK(sb_), "cbias"], writes=[pk_])
                        return (p_, pk_, c0, N)

                    def emit_av(h, Q, m, kt, blk):
                        p_, pk_, c0, N = blk
                        Ob, Db = 2 + m, 4 + m
                        nkt = 4 * Q + 4
                        P.op("pe", lambda: nc.tensor.matmul(
                            ps[Ob][:, c0:CH], lhsT=Vt[:, kt, h * 128:(h + 1) * 128], rhs=p_[:, 0:N], start=(kt == 0), stop=(kt == nkt - 1)),
                            reads=[("Vt", kt), pk_], writes=[PK(Ob)], signal=False)
                        P.op("pe", lambda: nc.tensor.matmul(
                            ps[Db][:, c0:CH], lhsT=ones_b[:], rhs=p_[:, 0:N], start=(kt == 0), stop=(kt == nkt - 1)),
                            reads=["ones_b", pk_], writes=[PK(Db)], signal=True)

                    def epilogue_a(h, Q):
                        P.op("act", lambda: nc.scalar.activation(out=fr[:], in_=ps[4][:, :], func=AF.Ln), reads=[PK(4)], writes=["fr"])
                        P.op("act", lambda: nc.scalar.activation(out=fb[:], in_=ps[5][:, :], func=AF.Ln), reads=[PK(5)], writes=["fb"])
                        P.op("act", lambda: nc.scalar.activation(out=fr[:], in_=fr[:], func=AF.Exp, scale=-1.0), reads=["fr"], writes=["fr"])
                        P.op("dve", lambda: nc.vector.tensor_tensor(out=fa[:], in0=ps[2][:, :], in1=fr[:], op=ALU.mult), reads=[PK(2), "fr"], writes=["fa"])
                        P.op("act", lambda: nc.scalar.activation(out=fb[:], in_=fb[:], func=AF.Exp, scale=-1.0), reads=["fb"], writes=["fb"])
                        P.op("dve", lambda: nc.vector.tensor_tensor(out=fb[:], in0=ps[3][:, :], in1=fb[:], op=ALU.mult), reads=[PK(3), "fb"], writes=["fb"])
                        P.op("dve", lambda: nc.vector.scalar_tensor_tensor(out=fo[:], in0=fb[:], scalar=neglam[:, 0:1], in1=fa[:], op0=ALU.mult, op1=ALU.add),
                             reads=["fa", "fb", "neglam"], writes=["fo"])
                        P.op("dve", lambda: nc.vector.tensor_tensor(out=fsq[:], in0=fo[:], in1=fo[:], op=ALU.mult), reads=["fo"], writes=["fsq"])

                    def epilogue_b(h, Q):
                        cs = slice(Q * CH, (Q + 1) * CH)
                        P.op("pe", lambda: nc.tensor.matmul(ps[6][:, :], lhsT=ones_b[:], rhs=fsq[:], start=True, stop=True), reads=["fsq", "ones_b"], writes=[PK(6)])
                        P.op("act", lambda: nc.scalar.activation(out=fa[:], in_=ps[6][:, :], func=AF.Ln, scale=1.0 / 128, bias=epsb[:, 0:1]),
                             reads=[PK(6), "epsb"], writes=["fa"])
                        P.op("act", lambda: nc.scalar.activation(out=fa[:], in_=fa[:], func=AF.Exp, scale=-0.5), reads=["fa"], writes=["fa"])
                        P.op("dve", lambda: nc.vector.scalar_tensor_tensor(out=oT[:, h, cs], in0=fo[:], scalar=sublw[:, 0:1], in1=fa[:],
                                                                           op0=ALU.mult, op1=ALU.mult),
                             reads=["fo", "fa", "sublw"], writes=[("oT", "all")])

                    for h in range(NH):
                        for Q in range(NCH):
                            blocks = [(m, kt) for m in range(2) for kt in range(4 * Q + 4)]
                            pend = []
                            for bi, (m, kt) in enumerate(blocks):
                                cur = emit_qk(h, Q, m, kt)
                                pend.append((m, kt, cur))
                                if len(pend) > 2:
                                    pm, pkt, pblk = pend.pop(0)
                                    emit_av(h, Q, pm, pkt, pblk)
                                if bi == 3 and deferred:
                                    deferred.pop(0)()
                            while pend:
                                pm, pkt, pblk = pend.pop(0)
                                emit_av(h, Q, pm, pkt, pblk)
                            while deferred:
                                deferred.pop(0)()
                            epilogue_a(h, Q)
                            deferred.append(lambda h=h, Q=Q: epilogue_b(h, Q))
                    while deferred:
                        deferred.pop(0)()
                    P.barrier()

                checkpoint(10 * l + 6, [stB])
                with ExitStack() as st:
                    mixT = sb("mixT" + L, [128, 8, S], BF16, st)
                    wsl = [sb("wsM%d%s" % (i, L), [128, 8, 256], BF16, st) for i in range(4)]
                    wkeys = ["wsM%d" % i for i in range(4)]
                    wsl2 = [sb("wsN%d%s" % (i, L), [128, 4, 256], BF16, st) for i in range(4)]
                    wkeys2 = ["wsN%d" % i for i in range(4)]
                    sg = [sb("msg%d%s" % (i, L), [128, CH], F32, st) for i in range(2)]
                    m1 = sb("m1" + L, [128, CH], F32, st)
                    m2 = sb("m2" + L, [128, CH], F32, st)
                    wst = {"i": 0}
                    wst2 = {"i": 0}
                    for cc in range(4):
                        wga, kga = load_w(wsl, wkeys, w_in_l[:, 2048 + cc * 256:2048 + (cc + 1) * 256], 8, 256, wst)
                        wgb, kgb = load_w(wsl, wkeys, w_in_l[:, 3072 + cc * 256:3072 + (cc + 1) * 256], 8, 256, wst)
                        wa, ka = load_w(wsl2, wkeys2, p["w_a"][l][:, cc * 256:(cc + 1) * 256], 4, 256, wst2)
                        wb, kb = load_w(wsl2, wkeys2, p["w_b"][l][:, cc * 256:(cc + 1) * 256], 4, 256, wst2)
                        for hh in range(2):
                            mt = cc * 2 + hh
                            ms = slice(hh * 128, (hh + 1) * 128)
                            for c in range(NCH):
                                cs = slice(c * CH, (c + 1) * CH)
                                pb = 4 * (c % 2)
                                for kt in range(8):
                                    P.op("pe", lambda kt=kt, cs=cs, pb=pb, ms=ms, wga=wga: nc.tensor.matmul(ps[pb][:, :], lhsT=wga[:, kt, ms], rhs=hT[:, kt, cs],
                                                                                                          start=(kt == 0), stop=(kt == 7)),
                                         reads=[kga, ("hT", c)], writes=[PK(pb)], signal=(kt == 7))
                                for kt in range(8):
                                    P.op("pe", lambda kt=kt, cs=cs, pb=pb, ms=ms, wgb=wgb: nc.tensor.matmul(ps[pb + 1][:, :], lhsT=wgb[:, kt, ms], rhs=hT[:, kt, cs],
                                                                                                          start=(kt == 0), stop=(kt == 7)),
                                         reads=[kgb, ("hT", c)], writes=[PK(pb + 1)], signal=(kt == 7))
                                for kt in range(4):
                                    P.op("pe", lambda kt=kt, cs=cs, pb=pb, ms=ms, wa=wa: nc.tensor.matmul(ps[pb + 2][:, :], lhsT=wa[:, kt, ms], rhs=oT[:, kt, cs],
                                                                                                        start=(kt == 0), stop=(kt == 3)),
                                         reads=[ka, ("oT", "all")], writes=[PK(pb + 2)], signal=(kt == 3))
                                for kt in range(4):
                                    P.op("pe", lambda kt=kt, cs=cs, pb=pb, ms=ms, wb=wb: nc.tensor.matmul(ps[pb + 3][:, :], lhsT=wb[:, kt, ms], rhs=sT[:, kt, cs],
                                                                                                        start=(kt == 0), stop=(kt == 3)),
                                         reads=[kb, ("sT", "all")], writes=[PK(pb + 3)], signal=(kt == 3))
                                P.op("act", lambda pb=pb: nc.scalar.activation(out=sg[0][:], in_=ps[pb][:, :], func=AF.Sigmoid), reads=[PK(pb)], writes=[("msg", 0)])
                                P.op("act", lambda pb=pb: nc.scalar.activation(out=sg[1][:], in_=ps[pb + 1][:, :], func=AF.Sigmoid), reads=[PK(pb + 1)], writes=[("msg", 1)])
                                P.op("dve", lambda pb=pb: nc.vector.tensor_tensor(out=m1[:], in0=ps[pb + 2][:, :], in1=sg[0][:], op=ALU.mult),
                                     reads=[PK(pb + 2), ("msg", 0)], writes=["m1"])
                                P.op("dve", lambda pb=pb: nc.vector.tensor_tensor(out=m2[:], in0=ps[pb + 3][:, :], in1=sg[1][:], op=ALU.mult),
                                     reads=[PK(pb + 3), ("msg", 1)], writes=["m2"])
                                P.op("dve", lambda mt=mt, cs=cs: nc.vector.tensor_tensor(out=mixT[:, mt, cs], in0=m1[:], in1=m2[:], op=ALU.add),
                                     reads=["m1", "m2"], writes=[("mixT", c)])
                    for cc in range(4):
                        wo, ko = load_w(wsl, wkeys, p["w_o"][l][:, cc * 256:(cc + 1) * 256], 8, 256, wst)
                        for hh in range(2):
                            mt = cc * 2 + hh
                            ms = slice(hh * 128, (hh + 1) * 128)
                            for c in range(NCH):
                                cs = slice(c * CH, (c + 1) * CH)
                                pb = (mt * NCH + c) % 4
                                for kt in range(8):
                                    P.op("pe", lambda kt=kt, cs=cs, pb=pb, ms=ms, wo=wo: nc.tensor.matmul(ps[pb][:, :], lhsT=wo[:, kt, ms], rhs=mixT[:, kt, cs],
                                                                                                        start=(kt == 0), stop=(kt == 7)),
                                         reads=[ko, ("mixT", c)], writes=[PK(pb)], signal=(kt == 7))
                                P.op("dve", lambda mt=mt, cs=cs, pb=pb: nc.vector.tensor_tensor(out=xT[:, mt, cs], in0=xT[:, mt, cs], in1=ps[pb][:, :], op=ALU.add),
                                     reads=[PK(pb), ("xT", "all")], writes=[("xT", "all")])
                    P.barrier()

            checkpoint(10 * l + 7)
            rmsnorm_to_hT(p["norm_ffn"][l], es, "f" + L)
            with ExitStack() as st:
                act = sb("act" + L, [128, 12, S], BF16, st)
                wsl = [sb("wsF%d%s" % (i, L), [128, 12, 256], BF16, st) for i in range(4)]
                wkeys = ["wsF%d" % i for i in range(4)]
                sl = [sb("fsl%d%s" % (i, L), [128, CH], F32, st) for i in range(2)]
                wst = {"i": 0}
                it = 0
                for hf in range(2):
                    mt0 = 0 if hf == 0 else 12
                    nmt = 12 if hf == 0 else 10
                    for cc in range(nmt // 2):
                        col = (mt0 + cc * 2) * 128
                        w1t, k1 = load_w(wsl, wkeys, p["w1"][l][:, col:col + 256], 8, 256, wst)
                        w3t, k3 = load_w(wsl, wkeys, p["w3"][l][:, col:col + 256], 8, 256, wst)
                        for hh in range(2):
                            ml = cc * 2 + hh
                            ms = slice(hh * 128, (hh + 1) * 128)
                            for c in range(NCH):
                                cs = slice(c * CH, (c + 1) * CH)
                                pb = 2 * (it % 2)
                                sl_ = sl[it % 2]
                                slk = ("fsl", it % 2)
                                it += 1
                                for kt in range(8):
                                    P.op("pe", lambda kt=kt, cs=cs, pb=pb, ms=ms, w1t=w1t: nc.tensor.matmul(ps[pb][:, :], lhsT=w1t[:, kt, ms], rhs=hT[:, kt, cs],
                                                                                                          start=(kt == 0), stop=(kt == 7)),
                                         reads=[k1, ("hT", c)], writes=[PK(pb)], signal=(kt == 7))
                                for kt in range(8):
                                    P.op("pe", lambda kt=kt, cs=cs, pb=pb, ms=ms, w3t=w3t: nc.tensor.matmul(ps[pb + 1][:, :], lhsT=w3t[:, kt, ms], rhs=hT[:, kt, cs],
                                                                                                          start=(kt == 0), stop=(kt == 7)),
                                         reads=[k3, ("hT", c)], writes=[PK(pb + 1)], signal=(kt == 7))
                                P.op("act", lambda pb=pb, sl_=sl_: nc.scalar.activation(out=sl_[:], in_=ps[pb][:, :], func=AF.Silu), reads=[PK(pb)], writes=[slk])
                                P.op("dve", lambda pb=pb, sl_=sl_, ml=ml, cs=cs: nc.vector.tensor_tensor(out=act[:, ml, cs], in0=sl_[:], in1=ps[pb + 1][:, :], op=ALU.mult),
                                     reads=[slk, PK(pb + 1)], writes=[("act", c)])
                    for cc in range(4):
                        w2t, k2 = load_w(wsl, wkeys, p["w2"][l][mt0 * 128:(mt0 + nmt) * 128, cc * 256:(cc + 1) * 256], nmt, 256, wst)
                        for hh in range(2):
                            mo = cc * 2 + hh
                            ms = slice(hh * 128, (hh + 1) * 128)
                            for c in range(NCH):
                                cs = slice(c * CH, (c + 1) * CH)
                                pb = 4 + (mo * NCH + c) % 4
                                for kt in range(nmt):
                                    P.op("pe", lambda kt=kt, cs=cs, pb=pb, ms=ms, w2t=w2t, nmt=nmt: nc.tensor.matmul(ps[pb][:, :], lhsT=w2t[:, kt, ms], rhs=act[:, kt, cs],
                                                                                                                   start=(kt == 0), stop=(kt == nmt - 1)),
                                         reads=[k2, ("act", c)], writes=[PK(pb)], signal=(kt == nmt - 1))
                                P.op("dve", lambda mo=mo, cs=cs, pb=pb: nc.vector.tensor_tensor(out=xT[:, mo, cs], in0=xT[:, mo, cs], in1=ps[pb][:, :], op=ALU.add),
                                     reads=[PK(pb), ("xT", "all")], writes=[("xT", "all")])
                P.barrier()

        for l in range(DEPTH):
            try:
                _layer(l)
            except _Stop:
                break
            except BaseException:
                import traceback
                traceback.print_exc()
                raise

        with ExitStack() as st:
            xo = [sb("xo%d" % i, [128, D], F32, st) for i in range(2)]
            for tt in range(16):
                b_ = xo[tt % 2]
                for half in range(2):
                    pb = (tt * 2 + half) % 4
                    for q in range(4):
                        kt = half * 4 + q
                        P.op("pe", lambda kt=kt, q=q, pb=pb, tt=tt: nc.tensor.transpose(
                            ps[pb][:, q * 128:(q + 1) * 128], xT[:, kt, tt * 128:(tt + 1) * 128], ident_f[:]),
                            reads=[("xT", "all"), "ident_f"], writes=[PK(pb)])
                    if half == 0:
                        P.op("dve", lambda b_=b_, pb=pb: nc.vector.tensor_copy(out=b_[:, 0:512], in_=ps[pb][:, :]), reads=[PK(pb)], writes=[("xo", tt % 2, 0)])
                    else:
                        P.op("act", lambda b_=b_, pb=pb: nc.scalar.copy(out=b_[:, 512:1024], in_=ps[pb][:, :]), reads=[PK(pb)], writes=[("xo", tt % 2, 1)])
                P.dma("sp", "xo%d" % (tt % 2), y_d[tt * 128:(tt + 1) * 128, :], b_[:], reads=[("xo", tt % 2, 0), ("xo", tt % 2, 1)], writes=[("y", tt)])
            P.finish()
    return nc, consts


_CACHE = {}


def kernel(**inputs):
    if "nc" not in _CACHE:
        _CACHE["nc"] = build_program()
    nc, consts = _CACHE["nc"]
    x = np.ascontiguousarray(np.asarray(inputs["x"], dtype=np.float32))
    B = x.shape[0]
    shared = {}
    for k, v in inputs.items():
        if k == "x":
            continue
        shared[k] = np.ascontiguousarray(np.asarray(v, dtype=np.float32))
    for k, v in consts.items():
        shared["c_" + k] = v
    in_maps = []
    for b in range(B):
        m = dict(shared)
        m["x"] = x[b]
        in_maps.append(m)
    res = run_bass_kernel_spmd(nc, in_maps, core_ids=list(range(B)))
    out = np.stack([np.asarray(res.results[b]["y"], dtype=np.float32) for b in range(B)], axis=0)
    return out
```
